# Optimizing a Trainium2 kernel written in Bass

```python
import math
import jax, jax.numpy as jnp
from jax import lax
import numpy as np

D_MODEL = 1024
BATCH = 16
SEQ = 2048
DEPTH = 1

CHUNK = 64
N_MEM = 256
D_CONV = D_MODEL
CONV_WIDTH = 3
FOX_HEAD_DIM = 128
FOX_HEADS = D_MODEL // FOX_HEAD_DIM
FOX_WIDTH = FOX_HEADS * FOX_HEAD_DIM
XA_HEADS = 4
XA_HEAD_DIM = D_MODEL // XA_HEADS
XA_WIDTH = XA_HEADS * XA_HEAD_DIM
N_BRANCH = 3
D_FF = -(-8 * D_MODEL // (3 * 256)) * 256
Q_BLOCK = 128
EPS = 1e-6
IN_SPLITS = (D_CONV, D_CONV, D_CONV, FOX_WIDTH, FOX_WIDTH, FOX_WIDTH, XA_WIDTH,
             D_MODEL, D_MODEL, D_MODEL, FOX_HEADS)
IN_COLS = sum(IN_SPLITS)

kernel_name = "hybrid_conv_fox_memory_block"


def rms_norm(x, g):
    xf = x.astype(jnp.float32)
    y = xf * lax.rsqrt(jnp.mean(xf * xf, axis=-1, keepdims=True) + EPS)
    return (y * g.astype(jnp.float32)).astype(x.dtype)


def split_cols(z):
    outs, off = [], 0
    for w in IN_SPLITS:
        outs.append(z[..., off:off + w])
        off += w
    return outs


def short_gated_conv(b_gate, c_gate, v, conv_w, conv_b):
    u = c_gate * v
    rhs = conv_w[:, None, :]
    y = lax.conv_general_dilated(u, rhs, window_strides=(1,),
                                 padding=[(CONV_WIDTH - 1, 0)],
                                 dimension_numbers=('NWC', 'WIO', 'NWC'),
                                 feature_group_count=D_CONV)
    return b_gate * (y + conv_b)


def forgetting_attention(q, k, v, log_f):
    b, s, h, hd = q.shape
    nb = s // Q_BLOCK
    c = jnp.cumsum(log_f, axis=1).transpose(0, 2, 1)
    kh = k.transpose(0, 2, 1, 3)
    vh = v.transpose(0, 2, 1, 3)
    q_blocks = q.transpose(0, 2, 1, 3).reshape(b, h, nb, Q_BLOCK, hd).transpose(2, 0, 1, 3, 4)
    c_blocks = c.reshape(b, h, nb, Q_BLOCK).transpose(2, 0, 1, 3)
    k_pos = jnp.arange(s)
    scale = 1.0 / math.sqrt(hd)

    def block(args):
        qb, cb, i = args
        logits = jnp.einsum('bhqd,bhkd->bhqk', qb, kh,
                            preferred_element_type=jnp.float32) * scale
        logits = logits + cb[..., None] - c[:, :, None, :]
        q_pos = i * Q_BLOCK + jnp.arange(Q_BLOCK)
        mask = q_pos[:, None] >= k_pos[None, :]
        logits = jnp.where(mask, logits, -jnp.inf)
        p = jax.nn.softmax(logits, axis=-1).astype(vh.dtype)
        return jnp.einsum('bhqk,bhkd->bhqd', p, vh)

    out = lax.map(block, (q_blocks, c_blocks, jnp.arange(nb)))
    return out.transpose(1, 0, 3, 2, 4).reshape(b, s, h * hd)


def memory_cross_attention(q, mem_n, w_mem_kv, q_g, k_g):
    b, s = q.shape[0], q.shape[1]
    kv = mem_n @ w_mem_kv
    k = kv[..., :XA_WIDTH].reshape(b, -1, XA_HEADS, XA_HEAD_DIM)
    v = kv[..., XA_WIDTH:].reshape(b, -1, XA_HEADS, XA_HEAD_DIM)
    q = rms_norm(q, q_g)
    k = rms_norm(k, k_g)
    logits = jnp.einsum('bshd,bmhd->bhsm', q, k,
                        preferred_element_type=jnp.float32) / math.sqrt(XA_HEAD_DIM)
    p = jax.nn.softmax(logits, axis=-1).astype(v.dtype)
    return jnp.einsum('bhsm,bmhd->bshd', p, v).reshape(b, s, XA_WIDTH)


def setup_inputs(seed: int = 0) -> dict:
    key = jax.random.key(seed)
    ks = jax.random.split(key, 24)
    f32 = jnp.float32
    nrm = lambda k, shape, fan: jax.random.normal(k, shape, f32) * fan ** -0.5
    gain = lambda k, shape: 1.0 + 0.02 * jax.random.normal(k, shape, f32)
    L = DEPTH
    return {
        "x": jax.random.normal(ks[0], (BATCH, SEQ, D_MODEL), f32),
        "mem": jax.random.normal(ks[1], (BATCH, N_MEM, D_MODEL), f32),
        "norm1_g": gain(ks[2], (L, D_MODEL)),
        "w_in": nrm(ks[3], (L, D_MODEL, IN_COLS), D_MODEL),
        "conv_w": nrm(ks[4], (L, CONV_WIDTH, D_CONV), CONV_WIDTH),
        "conv_b": 0.02 * jax.random.normal(ks[5], (L, D_CONV), f32),
        "fox_f_bias": 2.0 + 2.0 * jax.random.uniform(ks[6], (L, FOX_HEADS), f32),
        "fox_q_g": gain(ks[7], (L, FOX_HEAD_DIM)),
        "fox_k_g": gain(ks[8], (L, FOX_HEAD_DIM)),
        "mem_norm_g": gain(ks[9], (L, D_MODEL)),
        "w_mem_kv": nrm(ks[10], (L, D_MODEL, 2 * XA_WIDTH), D_MODEL),
        "xa_q_g": gain(ks[11], (L, XA_HEAD_DIM)),
        "xa_k_g": gain(ks[12], (L, XA_HEAD_DIM)),
        "w_br_conv": nrm(ks[13], (L, D_CONV, D_MODEL), D_CONV),
        "w_br_fox": nrm(ks[14], (L, FOX_WIDTH, D_MODEL), FOX_WIDTH),
        "w_br_xa": nrm(ks[15], (L, XA_WIDTH, D_MODEL), XA_WIDTH),
        "w_o": nrm(ks[16], (L, D_MODEL, D_MODEL), D_MODEL),
        "norm2_g": gain(ks[17], (L, D_MODEL)),
        "w_ffn_in": nrm(ks[18], (L, D_MODEL, 2 * D_FF), D_MODEL),
        "w_ffn_out": nrm(ks[19], (L, D_FF, D_MODEL), D_FF),
    }


def reference(x, mem, norm1_g, w_in, conv_w, conv_b, fox_f_bias, fox_q_g, fox_k_g,
              mem_norm_g, w_mem_kv, xa_q_g, xa_k_g, w_br_conv, w_br_fox, w_br_xa,
              w_o, norm2_g, w_ffn_in, w_ffn_out):
    b, s, _ = x.shape
    for l in range(DEPTH):
        h = rms_norm(x, norm1_g[l])
        z = h @ w_in[l]
        (cb, cc, cv, fq, fk, fv, xq, ga, gb, gc, ff) = split_cols(z)

        y_conv = short_gated_conv(cb, cc, cv, conv_w[l], conv_b[l])

        fq = rms_norm(fq.reshape(b, s, FOX_HEADS, FOX_HEAD_DIM), fox_q_g[l])
        fk = rms_norm(fk.reshape(b, s, FOX_HEADS, FOX_HEAD_DIM), fox_k_g[l])
        fv = fv.reshape(b, s, FOX_HEADS, FOX_HEAD_DIM)
        log_f = jax.nn.log_sigmoid(ff.astype(jnp.float32) + fox_f_bias[l].astype(jnp.float32))
        y_fox = forgetting_attention(fq, fk, fv, log_f)

        mem_n = rms_norm(mem, mem_norm_g[l])
        y_xa = memory_cross_attention(xq.reshape(b, s, XA_HEADS, XA_HEAD_DIM), mem_n,
                                      w_mem_kv[l], xa_q_g[l], xa_k_g[l])

        merged = (jax.nn.sigmoid(ga) * (y_conv @ w_br_conv[l])
                  + jax.nn.sigmoid(gb) * (y_fox @ w_br_fox[l])
                  + jax.nn.sigmoid(gc) * (y_xa @ w_br_xa[l]))
        x = x + merged @ w_o[l]

        h2 = rms_norm(x, norm2_g[l])
        gu = h2 @ w_ffn_in[l]
        x = x + (jax.nn.silu(gu[..., :D_FF]) * gu[..., D_FF:]) @ w_ffn_out[l]
    return x
```

```python
import math
from contextlib import ExitStack

import numpy as np
import concourse.bass as bass
import concourse.mybir as mybir
from concourse.bass_utils import run_bass_kernel_spmd

F32 = mybir.dt.float32
BF16 = mybir.dt.bfloat16
AF = mybir.ActivationFunctionType
ALU = mybir.AluOpType

N_CORES = 8
D = 1024
S = 2048
NB = 2
NT = S // 128
KC = D // 128
NMEM = 256
DFF = 2816
NF = DFF // 128
IN_COLS = 10248
OFF_CB, OFF_CC, OFF_CV, OFF_FQ, OFF_FK, OFF_FV, OFF_XQ, OFF_GA, OFF_GB, OFF_GC, OFF_FF = (
    0, 1024, 2048, 3072, 4096, 5120, 6144, 7168, 8192, 9216, 10240)
EPS = 1e-6
NEG = -30000.0


class Tok:
    __slots__ = ("sem", "val")

    def __init__(self, sem, val):
        self.sem = sem
        self.val = val


class Buf:
    __slots__ = ("w", "r")

    def __init__(self, init=()):
        self.w = None
        self.r = {}
        for t in init:
            self.add_r(t)

    def add_r(self, t):
        if t is None:
            return
        k = id(t.sem)
        o = self.r.get(k)
        if o is None or o.val < t.val:
            self.r[k] = t

    def all_toks(self):
        return ([self.w] if self.w is not None else []) + list(self.r.values())


def retire(bufs):
    out = Buf()
    for b in bufs:
        for t in b.all_toks():
            out.add_r(t)
    return list(out.r.values())


class Eng:
    def __init__(self, name, sem, skip_self=False):
        self.name = name
        self.sem = sem
        self.cnt = 0
        self.ops = []
        self.waited = {}
        self.skip_self = skip_self

    def wait(self, toks):
        for t in toks:
            if t is None:
                continue
            if self.skip_self and t.sem is self.sem:
                continue
            k = id(t.sem)
            if self.waited.get(k, 0) >= t.val:
                continue
            self.waited[k] = t.val
            self.ops.append(lambda e, t=t: e.wait_ge(t.sem, t.val))

    @staticmethod
    def _deps(outs, ins, extra):
        deps = list(extra)
        for b in ins:
            deps.append(b.w)
        for b in outs:
            deps.extend(b.all_toks())
        return deps

    @staticmethod
    def _reg(tok, outs, ins):
        for b in ins:
            b.add_r(tok)
        for b in outs:
            b.w = tok
            b.r = {}

    def do(self, fn, outs=(), ins=(), extra=()):
        self.wait(self._deps(outs, ins, extra))
        self.cnt += 1
        tok = Tok(self.sem, self.cnt)
        sem = self.sem
        self.ops.append(lambda e: fn(e).then_inc(sem, 1))
        self._reg(tok, outs, ins)
        return tok

    def group(self, fns, outs=(), ins=(), extra=()):
        self.wait(self._deps(outs, ins, extra))
        for fn in fns[:-1]:
            self.ops.append(lambda e, fn=fn: fn(e))
        self.cnt += 1
        tok = Tok(self.sem, self.cnt)
        sem = self.sem
        last = fns[-1]
        self.ops.append(lambda e: last(e).then_inc(sem, 1))
        self._reg(tok, outs, ins)
        return tok

    def dma(self, fn, slot, outs=(), ins=(), extra=()):
        self.wait(self._deps(outs, ins, extra))
        slot[1] += 16
        tok = Tok(slot[0], slot[1])
        s = slot[0]
        self.ops.append(lambda e: fn(e).then_inc(s, 16))
        self._reg(tok, outs, ins)
        return tok

    def replay(self, e):
        for o in self.ops:
            o(e)


class Ring:
    def __init__(self, items):
        self.items = items
        self.i = 0

    def get(self):
        it = self.items[self.i % len(self.items)]
        self.i += 1
        return it


def build_program():
    nc = bass.Bass("TRN2", target_bir_lowering=False)

    def din(name, shape):
        return nc.dram_tensor(name, shape, F32, kind="ExternalInput").ap()

    x_d = din("x", [NB * S, D])
    mem_d = din("mem", [NB * NMEM, D])
    n1g_d = din("norm1_g", [1, D])
    w_in = din("w_in", [D, IN_COLS])
    conv_w_d = din("conv_w", [3, D])
    conv_b_d = din("conv_b", [1, D])
    fbias_d = din("fox_f_bias", [1, 8])
    fqg_d = din("fox_q_g", [1, 128])
    fkg_d = din("fox_k_g", [1, 128])
    mng_d = din("mem_norm_g", [1, D])
    wmkv = din("w_mem_kv", [D, 2 * D])
    xqg_d = din("xa_q_g", [1, 256])
    xkg_d = din("xa_k_g", [1, 256])
    wbr = {"conv": din("w_br_conv", [D, D]), "fox": din("w_br_fox", [D, D]), "xa": din("w_br_xa", [D, D])}
    wo_d = din("w_o", [D, D])
    n2g_d = din("norm2_g", [1, D])
    wfi = din("w_ffn_in", [D, 2 * DFF])
    wfo = din("w_ffn_out", [DFF, D])
    c_ident = din("c_ident", [128, 128])
    c_utri = din("c_utri", [128, 128])
    c_mjj = din("c_mjj", [128, 128])
    c_mjji = din("c_mjji", [128, 128])
    c_mneg = din("c_mneg", [128, 128])
    out_d = nc.dram_tensor("out", [NB * S, D], F32, kind="ExternalOutput").ap()

    with ExitStack() as es:
        def sb(name, shape, dt):
            return es.enter_context(nc.sbuf_tensor(name, shape, dt))

        def ps(name, shape, dt):
            return es.enter_context(nc.psum_tensor(name, shape, dt))

        def sem(name):
            return es.enter_context(nc.semaphore(name))

        PE = Eng("pe", sem("s_pe"), skip_self=True)
        ACT = Eng("act", sem("s_act"))
        DVE = Eng("dve", sem("s_dve"))
        POOL = Eng("pool", sem("s_pool"))
        SP = Eng("sp", sem("s_sp"))

        A = sb("A", [128, 16384], BF16)
        Bt = sb("B", [128, 16384], BF16)
        CD = sb("CD", [128, 24576], BF16)
        y_v = CD[:, 8192:24576].rearrange("p (a b) -> p a b", a=KC)
        aT_v = CD[:, 0:NF * 1024].rearrange("p (a b) -> p a b", a=NF)
        memT_v = CD[:, 0:2048].rearrange("p (a b) -> p a b", a=KC)

        NW = 5
        wbufs = [sb(f"w{i}", [128, KC, 512], BF16) for i in range(NW)]
        wring = Ring([(wbufs[i], Buf(), [sem(f"dw{i}"), 0]) for i in range(NW)])

        xts = [sb(f"xt{i}", [128, D], F32) for i in range(3)]
        xring = Ring([(xts[i], Buf(), [sem(f"dx{i}"), 0]) for i in range(3)])
        gt = sb("gt", [128, D], F32)
        gt_b = Buf()
        gt_slot = [sem("dgt"), 0]
        xnbs = [sb(f"xnb{i}", [128, D], BF16) for i in range(2)]
        xnring = Ring([(xnbs[i], Buf()) for i in range(2)])
        NTMP = 5
        tmps = [sb(f"tmp{i}", [128, 512], F32) for i in range(NTMP)]
        tring = Ring([(tmps[i], Buf()) for i in range(NTMP)])
        pbs = [sb(f"pb{i}", [128, 512], BF16) for i in range(4)]
        pring = Ring([(pbs[i], Buf()) for i in range(4)])
        sqs = [sb(f"sq{i}", [128, 512], BF16) for i in range(3)]
        sqring = Ring([(sqs[i], Buf()) for i in range(3)])
        us = [sb(f"u{i}", [128, 514], F32) for i in range(2)]
        uring = Ring([(us[i], Buf()) for i in range(2)])
        st_small = [sb(f"st{i}", [128, 4], F32) for i in range(4)]
        string = Ring([(st_small[i], Buf()) for i in range(4)])

        ident32 = sb("ident32", [128, 128], F32)
        utri32 = sb("utri32", [128, 128], F32)
        mjj32 = sb("mjj32", [128, 128], F32)
        mjji32 = sb("mjji32", [128, 128], F32)
        ones32 = sb("ones32", [128, 128], F32)
        ident_bf = sb("ident_bf", [128, 128], BF16)
        mneg_bf = sb("mneg_bf", [128, 128], BF16)
        ones_bf = sb("ones_bf", [128, 128], BF16)
        rows = sb("rows", [38, 128], F32)
        cols = sb("cols", [128, 38], F32)
        fb_b = sb("fb_b", [128, NT, 8], F32)
        xakv = sb("xakv", [128, 4096], BF16)
        xa_kT = xakv[:, 0:2048].rearrange("p (a b) -> p a b", a=KC)
        xa_V = xakv[:, 2048:4096].rearrange("p (a b) -> p a b", a=2)
        wx_v = xakv[:].rearrange("p (a b) -> p a b", a=KC)
        wx_b = Buf()
        wx_slot = [sem("dwx"), 0]
        bias_all = sb("bias_all", [128, 40, 8], F32)
        wf_sb = sb("wf_sb", [128, KC, 8], BF16)
        const_b = Buf()
        cols_b = Buf()
        xak_b = Buf()
        xav_b = Buf()
        bias_b = Buf()
        wf_b = Buf()
        wf_slot = [sem("dwf"), 0]

        banks = [ps(f"bank{i}", [128, 512], F32) for i in range(7)]
        bank_tr = ps("bank_tr", [128, KC, 128], BF16)
        bk = [(banks[i], Buf()) for i in range(7)]
        tr_b = Buf()
        ringP = Ring(bk[0:4])
        ringQ = Ring(bk[4:7])
        ringS = Ring(bk[0:3])
        ringO = Ring(bk[3:5])
        ringL = Ring(bk[5:7])

        hT_t = [Buf() for _ in range(NT)]
        B_V_t = [Buf() for _ in range(NT)]
        B_M_t = {}
        kq_t = [Buf() for _ in range(4)]
        y_t = {(f, T): Buf() for f in range(KC) for T in range(4)}
        memT_b = Buf()
        aT_t = {}
        x1d_t = {}
        out_toks = []

        wheld = [False] * NW
        wpos = [0]

        wfreed_at = [0] * NW
        wclock = [0]

        def wnfree():
            return sum(1 for h_ in wheld if not h_)

        def wget():
            cands = [i for i in range(NW) if not wheld[i]]
            if not cands:
                raise RuntimeError("no free weight buffer")
            i = min(cands, key=lambda j: wfreed_at[j])
            wheld[i] = True
            wb, b, slot = wring.items[i]
            return wb, b, slot, i

        def wfree(w):
            assert wheld[w[2]]
            wheld[w[2]] = False
            wclock[0] += 1
            wfreed_at[w[2]] = wclock[0]

        def wload(src, kcn=KC, ncols=512):
            wb, b, slot, i = wget()
            dst = wb[:, 0:kcn, 0:ncols]
            s = src.rearrange("(kc p) n -> p kc n", p=128)
            POOL.dma(lambda e: e.dma_start(out=dst, in_=s), slot, outs=[b])
            return wb, b, i

        def wload3(c0s, ncols=128):
            wb, b, slot, i = wget()
            first = True
            for j, c0 in enumerate(c0s):
                dst = wb[:, :, j * ncols:(j + 1) * ncols]
                s = w_in[:, c0:c0 + ncols].rearrange("(kc p) n -> p kc n", p=128)
                if first:
                    POOL.dma(lambda e, dst=dst, s=s: e.dma_start(out=dst, in_=s), slot, outs=[b])
                    first = False
                else:
                    slot[1] += 16
                    tok = Tok(slot[0], slot[1])
                    sl = slot[0]
                    POOL.ops.append(lambda e, dst=dst, s=s: e.dma_start(out=dst, in_=s).then_inc(sl, 16))
                    b.w = tok
            return wb, b, i

        def mm(out_ap, pairs, outs, ins, start=True, stop=True):
            n = len(pairs)
            fns = []
            for i, (l, r) in enumerate(pairs):
                fns.append(lambda e, l=l, r=r, i=i: e.matmul(out_ap, lhsT=l, rhs=r,
                                                             start=(start and i == 0), stop=(stop and i == n - 1)))
            return PE.group(fns, outs=outs, ins=ins)

        def act(out, in_, func, outs, ins, **kw):
            return ACT.do(lambda e: e.activation(out=out, in_=in_, func=func, **kw), outs=outs, ins=ins)

        def rstd_small(ssq_ap, ssq_b, n):
            st, stb = string.get()
            act(st[:, 0:1], ssq_ap, AF.Ln, [stb], [ssq_b], scale=1.0 / n, bias=EPS)
            act(st[:, 1:2], st[:, 0:1], AF.Exp, [stb], [stb], scale=-0.5)
            return st[:, 1:2], stb

        def load_gain(src):
            SP.dma(lambda e: e.dma_start(out=gt[:], in_=src.partition_broadcast(128)), gt_slot, outs=[gt_b])

        def norm_tile(xt, xtb, dstT, dst_bufs, ncol0):
            xn, xnb = norm_A(xt, xtb)
            norm_B(xn, xnb, dstT, dst_bufs, ncol0)

        junk_ap = us[0][:].bitcast(BF16)[:, 0:D]
        junk_b = uring.items[0][1]

        def norm_A1(xt, xtb):
            st, stb = string.get()
            act(junk_ap, xt[:], AF.Square, [junk_b, stb], [xtb], accum_out=st[:, 2:3])
            act(st[:, 0:1], st[:, 2:3], AF.Ln, [stb], [stb], scale=1.0 / D, bias=EPS)
            act(st[:, 1:2], st[:, 0:1], AF.Exp, [stb], [stb], scale=-0.5)
            return st, stb

        def norm_A2(xt, xtb, st, stb):
            xn, xnb = xnring.get()
            DVE.do(lambda e: e.scalar_tensor_tensor(out=xn[:], in0=xt[:], scalar=st[:, 1:2], in1=gt[:],
                                                    op0=ALU.mult, op1=ALU.mult),
                   outs=[xnb], ins=[xtb, stb, gt_b])
            return xn, xnb

        def norm_A(xt, xtb):
            st, stb = norm_A1(xt, xtb)
            return norm_A2(xt, xtb, st, stb)

        def norm_B(xn, xnb, dstT, dst_bufs, ncol0):
            fns = [lambda e, kc=kc: e.transpose(out=bank_tr[:, kc, :], in_=xn[:, kc * 128:(kc + 1) * 128],
                                                identity=ident_bf[:]) for kc in range(KC)]
            PE.group(fns, outs=[tr_b], ins=[xnb, const_b])
            act(dstT[:, :, ncol0:ncol0 + 128], bank_tr[:], AF.Copy, dst_bufs, [tr_b])

        cslotA = [sem("dconstA"), 0]
        cslotB = [sem("dconstB"), 0]
        const2_b = Buf()
        rows_b = const2_b
        SP.dma(lambda e: e.dma_start(out=ident32[:], in_=c_ident), cslotA, outs=[const_b])
        mstage, mstage_b = tring.get()
        cdmas = [(utri32[:], c_utri), (mjj32[:], c_mjj), (mjji32[:], c_mjji),
                 (mstage[:, 0:128], c_mneg),
                 (rows[0:24, :], conv_w_d.rearrange("k (f p) -> (k f) p", p=128)),
                 (rows[24:32, :], conv_b_d.rearrange("o (f p) -> (o f) p", p=128)),
                 (rows[32:33, :], fqg_d), (rows[33:34, :], fkg_d),
                 (rows[34:36, :], xqg_d.rearrange("o (c p) -> (o c) p", p=128)),
                 (rows[36:38, :], xkg_d.rearrange("o (c p) -> (o c) p", p=128)),
                 (fb_b[:], bass.AP(fbias_d.tensor, 0, [[0, 128], [0, NT], [1, 8]]))]
        deferred_const = []
        for dst, src in cdmas:
            cslotB[1] += 16
            slB = cslotB[0]
            deferred_const.append(lambda e, dst=dst, src=src: e.dma_start(out=dst, in_=src).then_inc(slB, 16))
        const2_b.w = Tok(cslotB[0], cslotB[1])
        mstage_b.w = const2_b.w
        DVE.do(lambda e: e.tensor_copy(out=ident_bf[:], in_=ident32[:]), outs=[const_b], ins=[const_b])
        DVE.do(lambda e: e.memset(ones32[:], 1.0), outs=[const_b], ins=[const_b])
        DVE.do(lambda e: e.memset(ones_bf[:], 1.0), outs=[const_b], ins=[const_b])
        DVE.do(lambda e: e.memset(bias_all[:], 0.0), outs=[bias_b], ins=[])

        def late_consts():
            for fn in deferred_const:
                POOL.ops.append(fn)

        def late_consts_finish():
            DVE.do(lambda e: e.tensor_copy(out=mneg_bf[:], in_=mstage[:, 0:128]), outs=[const2_b], ins=[const2_b, mstage_b])
            pb0, pb0b = ringQ.get()
            mm(pb0[:, 0:38], [(rows[0:38, :], ident32[0:38, 0:38])], [pb0b], [rows_b, const_b])
            DVE.do(lambda e: e.tensor_copy(out=cols[:], in_=pb0[:, 0:38]), outs=[cols_b], ins=[pb0b])

        def col(i):
            return cols[:, i:i + 1]

        class Region:
            def __init__(self, t):
                self.t = t
                self.trk = {}

            def view(self, keys):
                init = retire(list(self.trk.values()))
                self.trk = {k: Buf(init=init) for k in keys}
                return self.trk

            def fm(self):
                return self.t[:].rearrange("p (a b) -> p a b", a=KC)

            def tm(self):
                return self.t[:].rearrange("p (a b) -> p a b", a=NT)

        regs = [Region(A), Region(Bt)]
        cd = {"kq": {(s_, T): Buf() for s_ in range(4) for T in range(4)}, "memT": Buf(),
              "y": {(f, T): Buf() for f in range(KC) for T in range(4)}, "aT": {}}

        class Ctx:
            pass

        def make_ctx(b):
            c = Ctx()
            c.b = b
            c.xrow0 = b * S
            c.H = regs[b % 2]
            c.G = regs[(b + 1) % 2]
            c.hT = None
            c.Vt = None
            c.Mt = None
            c.bidx = {}
            return c

        def run(gen):
            if gen is None:
                return
            for _ in gen:
                pass

        def interleave(main, side, every, lag=0):
            n = 0
            side_live = side is not None
            for _ in main:
                n += 1
                if side_live and n > lag and (n - lag) % every == 0:
                    try:
                        next(side)
                    except StopIteration:
                        side_live = False
            if side_live:
                for _ in side:
                    pass

        def chain(*gens):
            for g in gens:
                if g is None:
                    continue
                for x in g:
                    yield x

        def gen_M(c):
            b = c.b
            load_gain(mng_d)
            cd["memT"] = Buf(init=retire(list(cd["kq"].values())) + retire(list(cd["aT"].values())))
            memT_b = cd["memT"]
            for t_ in wx_b.all_toks():
                xak_b.add_r(t_)
                xav_b.add_r(t_)
            wcur = wload(wmkv[:, 0:512])
            prevB = None
            for mt in range(2):
                xt, xtb, slot = xring.get()
                r0 = b * NMEM + mt * 128
                SP.dma(lambda e, xt=xt, r0=r0: e.dma_start(out=xt[:], in_=mem_d[r0:r0 + 128, :]), slot, outs=[xtb])
                xn, xnb = norm_A(xt, xtb)
                if prevB is not None:
                    norm_B(prevB[0], prevB[1], memT_v, [memT_b], prevB[2] * 128)
                prevB = (xn, xnb, mt)
                yield
            norm_B(prevB[0], prevB[1], memT_v, [memT_b], prevB[2] * 128)
            yield
            rP, rQ = Ring(bk[4:6]), Ring(bk[6:7])
            for hx in range(4):
                pcs = []
                sqc = []
                for c_ in range(2):
                    f = 2 * hx + c_
                    wb, wbb = wcur[0:2]
                    pbk, pbb = rP.get()
                    mm(pbk[:, 0:NMEM], [(wb[:, kc, (f % 4) * 128:(f % 4 + 1) * 128], memT_v[:, kc, :]) for kc in range(KC)],
                       [pbb], [wbb, memT_b])
                    sq, sqb = sqring.get()
                    act(sq[:, 0:NMEM], pbk[:, 0:NMEM], AF.Square, [sqb], [pbb])
                    pcs.append((pbk, pbb))
                    sqc.append((sq, sqb))
                qb_, qbb = rQ.get()
                mm(qb_[:, 0:NMEM], [(ones_bf[:], sqc[c_][0][:, 0:NMEM]) for c_ in range(2)], [qbb],
                   [sqc[0][1], sqc[1][1], const_b])
                t1, t1b = tring.get()
                act(t1[:, 0:NMEM], qb_[:, 0:NMEM], AF.Ln, [t1b], [qbb], scale=1.0 / 256, bias=EPS)
                act(t1[:, 0:NMEM], t1[:, 0:NMEM], AF.Exp, [t1b], [t1b], scale=-0.5)
                for c_ in range(2):
                    f = 2 * hx + c_
                    pbk, pbb = pcs[c_]
                    DVE.do(lambda e, pbk=pbk, f=f, c_=c_, t1=t1: e.scalar_tensor_tensor(
                        out=xa_kT[:, f, :], in0=pbk[:, 0:NMEM], scalar=col(36 + c_), in1=t1[:, 0:NMEM],
                        op0=ALU.mult, op1=ALU.mult), outs=[xak_b], ins=[pbb, t1b, cols_b])
                if hx == 1:
                    wfree(wcur)
                    wcur = wload(wmkv[:, 512:1024])
                if hx == 3:
                    wfree(wcur)
                    wcur = wload(wmkv[:, D:D + 512])
                yield
            for half in range(2):
                for mt in range(2):
                    wb, wbb = wcur[0:2]
                    pbk, pbb = rP.get()
                    mm(pbk[:], [(memT_v[:, kc, mt * 128:(mt + 1) * 128], wb[:, kc, :]) for kc in range(KC)],
                       [pbb], [wbb, memT_b])
                    act(xa_V[:, mt, half * 512:(half + 1) * 512], pbk[:], AF.Copy, [xav_b], [pbb])
                    if half == 0 and mt == 1:
                        wfree(wcur)
                        wcur = wload(wmkv[:, D + 512:D + 1024])
                    yield
            wfree(wcur)

        def gen_N1(c):
            load_gain(n1g_d)
            c.hT = c.H.view(range(NT))
            hv = c.H.fm()
            prevB = None
            for t in range(NT):
                xt, xtb, slot = xring.get()
                r0 = c.xrow0 + t * 128
                SP.dma(lambda e, xt=xt, r0=r0: e.dma_start(out=xt[:], in_=x_d[r0:r0 + 128, :]), slot, outs=[xtb])
                if prevB is not None:
                    norm_B(prevB[0], prevB[1], hv, [c.hT[prevB[2]]], prevB[2] * 128)
                xn, xnb = norm_A(xt, xtb)
                prevB = (xn, xnb, t)
                yield
            norm_B(prevB[0], prevB[1], hv, [c.hT[prevB[2]]], prevB[2] * 128)
            yield

        def hT_T(c, T):
            return [c.hT[t] for t in range(4 * T, 4 * T + 4)]

        def gen_FV(c):
            c.Vt = c.G.view(range(NT))
            hv = c.H.fm()
            Vv = c.G.tm()
            wv2 = [wload(w_in[:, OFF_FV + cc_ * 512:OFF_FV + (cc_ + 1) * 512]) for cc_ in range(2)]
            s_wf = w_in[:, OFF_FF:OFF_FF + 8].rearrange("(kc p) n -> p kc n", p=128)
            POOL.dma(lambda e: e.dma_start(out=wf_sb[:], in_=s_wf), wf_slot, outs=[wf_b])
            fbk, fbb = ringQ.get()
            c.fbk = (fbk, fbb)
            for t in range(NT):
                for half in range(2):
                    wb, wbb = wv2[half][0:2]
                    pbk, pbb = ringP.get()
                    mm(pbk[:], [(hv[:, kc, t * 128:(t + 1) * 128], wb[:, kc, :]) for kc in range(KC)],
                       [pbb], [wbb, c.hT[t]])
                    if half == 0:
                        act(Vv[:, t, half * 512:(half + 1) * 512], pbk[:], AF.Copy, [c.Vt[t]], [pbb])
                    else:
                        DVE.do(lambda e, t=t, half=half, pbk=pbk: e.tensor_copy(
                            out=Vv[:, t, half * 512:(half + 1) * 512], in_=pbk[:]), outs=[c.Vt[t]], ins=[pbb])
                mm(fbk[:, t * 8:(t + 1) * 8], [(hv[:, kc, t * 128:(t + 1) * 128], wf_sb[:, kc, :]) for kc in range(KC)],
                   [fbb], [wf_b, c.hT[t]])
                if t == NT - 1:
                    for w_ in wv2:
                        wfree(w_)
                    fox_w["k"] = wload(w_in[:, OFF_FK:OFF_FK + 512])
                    fox_w["q"] = wload(w_in[:, OFF_FQ:OFF_FQ + 512])
                yield

        def gen_FV_final(c):
            fbk, fbb = c.fbk
            fv, fvb = tring.get()
            lf_sb = fv[:, 0:128]
            z_sb = fv[:, 128:256]
            c_sb = fv[:, 256:384]
            cend_sb = fv[:, 384:512]
            DVE.do(lambda e: e.tensor_tensor(out=lf_sb, in0=fbk[:, 0:128], in1=fb_b[:].rearrange("p a b -> p (a b)"),
                                             op=ALU.add), outs=[fvb], ins=[fbb, const_b, const2_b])
            act(lf_sb, lf_sb, AF.Exp, [fvb], [fvb], scale=-1.0)
            act(lf_sb, lf_sb, AF.Ln, [fvb], [fvb], bias=1.0)
            zbk, zbb = ringQ.get()
            mm(zbk[:, 0:128], [(lf_sb, ones32[:])], [zbb], [fvb, const_b])
            act(z_sb, zbk[:, 0:128], AF.Copy, [fvb], [zbb])
            cbk, cbb = ringQ.get()
            mm(cbk[:, 0:128], [(utri32[:], lf_sb), (z_sb, mjj32[:])], [cbb], [fvb, const_b, const2_b])
            DVE.do(lambda e: e.tensor_copy(out=c_sb, in_=cbk[:, 0:128]), outs=[fvb], ins=[cbb])
            ebk, ebb = ringQ.get()
            mm(ebk[:, 0:128], [(z_sb, mjji32[:])], [ebb], [fvb, const_b, const2_b])
            DVE.do(lambda e: e.tensor_copy(out=cend_sb, in_=ebk[:, 0:128]), outs=[fvb], ins=[ebb])
            k = 0
            for qb in range(4):
                for j in range(4 * qb + 4):
                    c.bidx[(qb, j)] = k
                    jr = 4 * qb + 1
                    DVE.do(lambda e, k=k, j=j, jr=jr: e.tensor_tensor(
                        out=bias_all[:, k, :], in0=c_sb[:, j * 8:(j + 1) * 8], in1=cend_sb[:, jr * 8:(jr + 1) * 8],
                        op=ALU.subtract), outs=[bias_b], ins=[fvb])
                    k += 1
            yield

        scale_fox = 1.0 / math.sqrt(128.0)
        fox_w = {}
        bank_tr32 = bank_tr[:].rearrange("p a b -> p (a b)").bitcast(F32)
        fox_rings = (ringS, ringO, ringL)
        xa_rings = (Ring(bk[0:4]), Ring(bk[4:6]), Ring(bk[6:7]))

        def fox_start(c):
            kqi = retire([cd["memT"]]) + retire(list(cd["kq"].values()))
            cd["kq"] = {(s_, T): Buf(init=kqi) for s_ in range(4) for T in range(4)}
            y_init = retire(list(cd["aT"].values()))
            cd["y"] = {(f, T): Buf(init=retire([cd["y"][(f, T)]]) + y_init) for f in range(KC) for T in range(4)}

        def gen_fox_units(c, h):
            hv = c.H.fm()
            rP, rQ = Ring(bk[0:4]), Ring([(bank_tr32, tr_b)])
            if h == 4:
                g = h // 4
                for w_ in fox_w.values():
                    wfree(w_)
                fox_w["k"] = wload(w_in[:, OFF_FK + g * 512:OFF_FK + (g + 1) * 512])
                fox_w["q"] = wload(w_in[:, OFF_FQ + g * 512:OFF_FQ + (g + 1) * 512])
            wk_c, wq_c = fox_w["k"][0:2], fox_w["q"][0:2]
            set_ = h % 2
            kslot, qslot = 2 * set_, 2 * set_ + 1
            kT = CD[:, kslot * 2048:(kslot + 1) * 2048]
            qT = CD[:, qslot * 2048:(qslot + 1) * 2048]
            units = [(which, T) for T in range(4) for which in (0, 1)]
            pend = None
            hl = h % 4
            for u in range(len(units) + 1):
                cur = None
                if u < len(units):
                    which, T = units[u]
                    wb, wbb = (wk_c, wq_c)[which]
                    pbk, pbb = rP.get()
                    mm(pbk[:], [(wb[:, kc, hl * 128:(hl + 1) * 128], hv[:, kc, T * 512:(T + 1) * 512]) for kc in range(KC)],
                       [pbb], [wbb] + hT_T(c, T))
                    sq, sqb = sqring.get()
                    act(sq[:], pbk[:], AF.Square, [sqb], [pbb])
                    cur = (which, T, pbk, pbb, sq, sqb)
                if pend is not None:
                    which, T, pbk, pbb, sq, sqb = pend
                    qb_, qbb = rQ.get()
                    mm(qb_[:], [(ones_bf[:], sq[:])], [qbb], [sqb, const_b])
                    t1, t1b = tring.get()
                    act(t1[:], qb_[:], AF.Ln, [t1b], [qbb], scale=1.0 / 128, bias=EPS)
                    act(t1[:], t1[:], AF.Exp, [t1b], [t1b], scale=-0.5)
                    dst = (kT, qT)[which][:, T * 512:(T + 1) * 512]
                    dstb = cd["kq"][((kslot, qslot)[which], T)]
                    gcol = col(33) if which == 0 else col(32)
                    DVE.do(lambda e, dst=dst, pbk=pbk, gcol=gcol, t1=t1: e.scalar_tensor_tensor(
                        out=dst, in0=pbk[:], scalar=gcol, in1=t1[:], op0=ALU.mult, op1=ALU.mult),
                        outs=[dstb], ins=[pbb, t1b, cols_b])
                pend = cur
                yield
            if h == 7:
                for w_ in fox_w.values():
                    wfree(w_)
                fox_w.clear()

        def gen_fox_attn(c, h):
            Vv = c.G.tm()
            set_ = h % 2
            kslot, qslot = 2 * set_, 2 * set_ + 1
            kT = CD[:, kslot * 2048:(kslot + 1) * 2048]
            qT = CD[:, qslot * 2048:(qslot + 1) * 2048]
            rS, rO, rL = fox_rings
            blocks = [(qb, j) for qb in range(4) for j in range(4 * qb + 4)]

            def qk(i):
                qb, j = blocks[i]
                r = j - 4 * qb
                c0 = max(r, 0) * 128
                sbk, sbb = rS.get()
                l_ap = kT[:, j * 128:(j + 1) * 128]
                r_ap = qT[:, qb * 512 + c0:(qb + 1) * 512]
                fns = [lambda e, sbk=sbk, c0=c0, l_ap=l_ap, r_ap=r_ap, r=r: e.matmul(
                    sbk[:, c0:512], lhsT=l_ap, rhs=r_ap, start=True, stop=(r < 0))]
                if r >= 0:
                    fns.append(lambda e, sbk=sbk, c0=c0: e.matmul(
                        sbk[:, c0:c0 + 128], lhsT=ident_bf[:], rhs=mneg_bf[:], start=False, stop=True))
                PE.group(fns, outs=[sbb], ins=[cd["kq"][(kslot, j // 4)], cd["kq"][(qslot, qb)], const_b, const2_b])
                return sbk, sbb, c0

            pend_qk = [qk(0), qk(1)]
            obk = obb = lbk = lbb = None
            for i, (qb, j) in enumerate(blocks):
                nj = 4 * qb + 4
                if j == 0:
                    obk, obb = rO.get()
                    lbk, lbb = rL.get()
                sbk, sbb, c0 = pend_qk.pop(0)
                if i + 2 < len(blocks):
                    pend_qk.append(qk(i + 2))
                p, pbuf = pring.get()
                bi = c.bidx[(qb, j)]
                act(p[:, c0:512], sbk[:, c0:512], AF.Exp, [pbuf], [sbb, bias_b],
                    scale=scale_fox, bias=bias_all[:, bi, h:h + 1])
                fns = [lambda e, obk=obk, c0=c0, j=j, p=p, nj=nj, h=h: e.matmul(
                           obk[:, c0:512], lhsT=Vv[:, j, h * 128:(h + 1) * 128], rhs=p[:, c0:512],
                           start=(j == 0), stop=(j == nj - 1)),
                       lambda e, lbk=lbk, c0=c0, j=j, p=p, nj=nj: e.matmul(
                           lbk[:, c0:512], lhsT=ones_bf[:], rhs=p[:, c0:512],
                           start=(j == 0), stop=(j == nj - 1))]
                PE.group(fns, outs=[obb, lbb], ins=[pbuf, c.Vt[j], const_b])
                if j == nj - 1:
                    t1, t1b = tring.get()
                    DVE.do(lambda e, t1=t1, lbk=lbk: e.tensor_copy(out=t1[:], in_=lbk[:]), outs=[t1b], ins=[lbb])
                    DVE.do(lambda e, t1=t1: e.reciprocal(out=t1[:], in_=t1[:]), outs=[t1b], ins=[t1b])
                    DVE.do(lambda e, t1=t1, obk=obk, qb=qb, h=h: e.tensor_tensor(
                        out=y_v[:, h, qb * 512:(qb + 1) * 512], in0=obk[:], in1=t1[:], op=ALU.mult),
                        outs=[cd["y"][(h, qb)]], ins=[obb, t1b])
                yield

        def gen_merge(c, name, gate_off, first):
            w = wbr[name]
            hv = c.H.fm()
            Mv = c.G.fm()
            if first:
                c.Mt = c.G.view([(f, T) for f in range(KC) for T in range(4)])
            nxt_w = None
            for g in range(2):
                if nxt_w is not None:
                    wbp_, wgp_ = nxt_w
                    nxt_w = None
                else:
                    wbp_ = wload(w[:, g * 512:(g + 1) * 512])
                    wgp_ = wload(w_in[:, gate_off + g * 512:gate_off + (g + 1) * 512])
                if g == 0 and wnfree() >= 3:
                    nxt_w = (wload(w[:, 512:1024]), wload(w_in[:, gate_off + 512:gate_off + 1024]))
                wbp, wgp = wbp_[0:2], wgp_[0:2]
                for fl in range(4):
                    f = 4 * g + fl
                    for T in range(4):
                        gbk, gbb = ringP.get()
                        mm(gbk[:], [(wgp[0][:, kc, fl * 128:(fl + 1) * 128], hv[:, kc, T * 512:(T + 1) * 512])
                                    for kc in range(KC)], [gbb], [wgp[1]] + hT_T(c, T))
                        sg, sgb = tring.get()
                        act(sg[:], gbk[:], AF.Sigmoid, [sgb], [gbb])
                        pbk, pbb = ringP.get()
                        mm(pbk[:], [(wbp[0][:, kc, fl * 128:(fl + 1) * 128], y_v[:, kc, T * 512:(T + 1) * 512])
                                    for kc in range(KC)], [pbb], [wbp[1]] + [cd["y"][(kc, T)] for kc in range(KC)])
                        mdst = Mv[:, f, T * 512:(T + 1) * 512]
                        mb = c.Mt[(f, T)]
                        if first:
                            DVE.do(lambda e, mdst=mdst, pbk=pbk, sg=sg: e.tensor_tensor(
                                out=mdst, in0=pbk[:], in1=sg[:], op=ALU.mult), outs=[mb], ins=[pbb, sgb])
                        else:
                            DVE.do(lambda e, pbk=pbk, sg=sg: e.tensor_tensor(
                                out=sg[:], in0=pbk[:], in1=sg[:], op=ALU.mult), outs=[sgb], ins=[pbb])
                            DVE.do(lambda e, mdst=mdst, sg=sg: e.tensor_tensor(
                                out=mdst, in0=sg[:], in1=mdst, op=ALU.add), outs=[mb], ins=[sgb])
                        yield
                wfree(wbp_)
                wfree(wgp_)

        def gen_conv(c):
            hv = c.H.fm()
            nxt_wc = None
            for f in range(KC):
                wc_ = nxt_wc if nxt_wc is not None else wload3([OFF_CC + f * 128, OFF_CV + f * 128, OFF_CB + f * 128])
                nxt_wc = None
                if f + 1 < KC and wnfree() >= 1:
                    nxt_wc = wload3([OFF_CC + (f + 1) * 128, OFF_CV + (f + 1) * 128, OFF_CB + (f + 1) * 128])
                wb, wbb = wc_[0:2]
                uprev = None
                for T in range(4):
                    cols_T = slice(T * 512, (T + 1) * 512)
                    ccbk, ccbb = ringP.get()
                    mm(ccbk[:], [(wb[:, kc, 0:128], hv[:, kc, cols_T]) for kc in range(KC)], [ccbb], [wbb] + hT_T(c, T))
                    ccs, ccsb = tring.get()
                    act(ccs[:], ccbk[:], AF.Copy, [ccsb], [ccbb])
                    cvbk, cvbb = ringP.get()
                    mm(cvbk[:], [(wb[:, kc, 128:256], hv[:, kc, cols_T]) for kc in range(KC)], [cvbb], [wbb] + hT_T(c, T))
                    u, ub = uring.get()
                    if uprev is None:
                        DVE.do(lambda e, u=u: e.memset(u[:, 0:2], 0.0), outs=[ub], ins=[])
                    else:
                        up, upb = uprev
                        DVE.do(lambda e, u=u, up=up: e.tensor_copy(out=u[:, 0:2], in_=up[:, 512:514]), outs=[ub], ins=[upb])
                    DVE.do(lambda e, u=u, cvbk=cvbk, ccs=ccs: e.tensor_tensor(
                        out=u[:, 2:514], in0=cvbk[:], in1=ccs[:], op=ALU.mult), outs=[ub], ins=[cvbb, ccsb])
                    uprev = (u, ub)
                    ta, tab = tring.get()
                    DVE.do(lambda e, ta=ta, u=u, f=f: e.tensor_scalar(
                        out=ta[:], in0=u[:, 2:514], scalar1=col(16 + f), scalar2=col(24 + f),
                        op0=ALU.mult, op1=ALU.add), outs=[tab], ins=[ub, cols_b])
                    DVE.do(lambda e, ta=ta, u=u, f=f: e.scalar_tensor_tensor(
                        out=ta[:], in0=u[:, 1:513], scalar=col(8 + f), in1=ta[:], op0=ALU.mult, op1=ALU.add),
                        outs=[tab], ins=[ub, cols_b])
                    DVE.do(lambda e, ta=ta, u=u, f=f: e.scalar_tensor_tensor(
                        out=ta[:], in0=u[:, 0:512], scalar=col(0 + f), in1=ta[:], op0=ALU.mult, op1=ALU.add),
                        outs=[tab], ins=[ub, cols_b])
                    cbbk, cbbb = ringP.get()
                    mm(cbbk[:], [(wb[:, kc, 256:384], hv[:, kc, cols_T]) for kc in range(KC)], [cbbb], [wbb] + hT_T(c, T))
                    DVE.do(lambda e, ta=ta, cbbk=cbbk, f=f, cols_T=cols_T: e.tensor_tensor(
                        out=y_v[:, f, cols_T], in0=cbbk[:], in1=ta[:], op=ALU.mult),
                        outs=[cd["y"][(f, T)]], ins=[cbbb, tab])
                    yield
                wfree(wc_)

        scale_xa = 1.0 / 16.0
        xa_w = {}

        def gen_xa_units(c, hx):
            hv = c.H.fm()
            rP, rQ = Ring(bk[0:6]), Ring([(bank_tr32, tr_b)])
            if hx % 2 == 0:
                for w_ in xa_w.values():
                    wfree(w_)
                xa_w["q"] = wload(w_in[:, OFF_XQ + (hx // 2) * 512:OFF_XQ + (hx // 2 + 1) * 512])
            wxq = xa_w["q"][0:2]
            set_ = hx % 2
            xq = CD[:, set_ * 4096:(set_ + 1) * 4096].rearrange("p (a b) -> p a b", a=2)
            def proj(T, c_):
                cols_T = slice(T * 512, (T + 1) * 512)
                fl = (2 * hx + c_) % 4
                pbk, pbb = rP.get()
                mm(pbk[:], [(wxq[0][:, kc, fl * 128:(fl + 1) * 128], hv[:, kc, cols_T]) for kc in range(KC)],
                   [pbb], [wxq[1]] + hT_T(c, T))
                sq, sqb = sqring.get()
                act(sq[:], pbk[:], AF.Square, [sqb], [pbb])
                return pbk, pbb, sq, sqb

            def finish(T, pcs):
                xqb = [cd["kq"][(2 * set_, T)], cd["kq"][(2 * set_ + 1, T)]]
                cols_T = slice(T * 512, (T + 1) * 512)
                qb_, qbb = rQ.get()
                mm(qb_[:], [(ones_bf[:], pcs[c_][2][:]) for c_ in range(2)], [qbb], [pcs[0][3], pcs[1][3], const_b])
                t1, t1b = tring.get()
                act(t1[:], qb_[:], AF.Ln, [t1b], [qbb], scale=1.0 / 256, bias=EPS)
                act(t1[:], t1[:], AF.Exp, [t1b], [t1b], scale=-0.5)
                for c_ in range(2):
                    pbk, pbb = pcs[c_][0:2]
                    DVE.do(lambda e, pbk=pbk, c_=c_, t1=t1, cols_T=cols_T: e.scalar_tensor_tensor(
                        out=xq[:, c_, cols_T], in0=pbk[:], scalar=col(34 + c_), in1=t1[:],
                        op0=ALU.mult, op1=ALU.mult), outs=[xqb[c_]], ins=[pbb, t1b, cols_b])

            pend = None
            for T in range(4):
                p0 = proj(T, 0)
                if pend is not None:
                    finish(T - 1, pend)
                p1 = proj(T, 1)
                pend = [p0, p1]
                yield
            finish(3, pend)
            yield
            if hx == 3:
                for w_ in xa_w.values():
                    wfree(w_)
                xa_w.clear()

        def gen_xa_attn(c, hx):
            set_ = hx % 2
            xq = CD[:, set_ * 4096:(set_ + 1) * 4096].rearrange("p (a b) -> p a b", a=2)
            rS, rO, rL = xa_rings

            def smm(T):
                xqb = [cd["kq"][(2 * set_, T)], cd["kq"][(2 * set_ + 1, T)]]
                cols_T = slice(T * 512, (T + 1) * 512)
                ps_ = []
                for mc in range(2):
                    sbk, sbb = rS.get()
                    mm(sbk[:], [(xa_kT[:, 2 * hx + c_, mc * 128:(mc + 1) * 128], xq[:, c_, cols_T]) for c_ in range(2)],
                       [sbb], [xak_b, xqb[0], xqb[1]])
                    p, pbuf = pring.get()
                    act(p[:], sbk[:], AF.Exp, [pbuf], [sbb], scale=scale_xa)
                    ps_.append((p, pbuf))
                return ps_

            cur = smm(0)
            for T in range(4):
                cols_T = slice(T * 512, (T + 1) * 512)
                ps_ = cur
                cur = smm(T + 1) if T + 1 < 4 else None
                lbk, lbb = rL.get()
                mm(lbk[:], [(ones_bf[:], ps_[mc][0][:]) for mc in range(2)], [lbb], [ps_[0][1], ps_[1][1], const_b])
                t1, t1b = tring.get()
                act(t1[:], lbk[:], AF.Ln, [t1b], [lbb])
                act(t1[:], t1[:], AF.Exp, [t1b], [t1b], scale=-1.0)
                evac = []
                for c_ in range(2):
                    f = 2 * hx + c_
                    obk, obb = rO.get()
                    mm(obk[:], [(xa_V[:, mc, f * 128:(f + 1) * 128], ps_[mc][0][:]) for mc in range(2)],
                       [obb], [xav_b, ps_[0][1], ps_[1][1]])
                    t2, t2b = tring.get()
                    DVE.do(lambda e, t2=t2, obk=obk: e.tensor_copy(out=t2[:], in_=obk[:]), outs=[t2b], ins=[obb])
                    evac.append((f, t2, t2b))
                for f, t2, t2b in evac:
                    DVE.do(lambda e, t1=t1, t2=t2, f=f, cols_T=cols_T: e.tensor_tensor(
                        out=y_v[:, f, cols_T], in0=t2[:], in1=t1[:], op=ALU.mult),
                        outs=[cd["y"][(f, T)]], ins=[t2b, t1b])
                yield

        wo_w = {}

        def gen_wo(c, tiles):
            b = c.b
            Mv = c.G.fm()
            hv = c.H.fm()
            load_gain(n2g_d)
            wo_c = [wload(wo_d[:, cc_ * 512:(cc_ + 1) * 512]) for cc_ in range(2)]

            def wo_load(t):
                xt, xtb, slot = xring.get()
                r0 = c.xrow0 + t * 128
                SP.dma(lambda e, xt=xt, r0=r0: e.dma_start(out=xt[:], in_=x_d[r0:r0 + 128, :]), slot, outs=[xtb])
                return xt, xtb, slot

            tiles = list(tiles)
            n = len(tiles)
            mm_res = {}
            xs = {}
            sts = {}
            xns = {}

            def do_mm(t):
                res = []
                for half in range(2):
                    pbk, pbb = ringP.get()
                    mm(pbk[:], [(Mv[:, kc, t * 128:(t + 1) * 128], wo_c[half][0][:, kc, :]) for kc in range(KC)],
                       [pbb], [wo_c[half][1]] + [c.Mt[(kc, t // 4)] for kc in range(KC)])
                    res.append((pbk, pbb))
                mm_res[t] = res

            def stage1(t):
                xt, xtb, slot = xs[t]
                r0 = c.xrow0 + t * 128
                for half in range(2):
                    pbk, pbb = mm_res[t][half]
                    DVE.do(lambda e, xt=xt, pbk=pbk, half=half: e.tensor_tensor(
                        out=xt[:, half * 512:(half + 1) * 512], in0=pbk[:], in1=xt[:, half * 512:(half + 1) * 512],
                        op=ALU.add), outs=[xtb], ins=[pbb])
                x1b = Buf()
                x1d_t[(b, t)] = x1b
                SP.dma(lambda e, xt=xt, r0=r0: e.dma_start(out=out_d[r0:r0 + 128, :], in_=xt[:]), slot,
                       outs=[x1b], ins=[xtb])
                sts[t] = norm_A1(xt, xtb)

            xs[tiles[0]] = wo_load(tiles[0])
            if n > 1:
                xs[tiles[1]] = wo_load(tiles[1])
            do_mm(tiles[0])
            if n > 1:
                do_mm(tiles[1])
            stage1(tiles[0])
            for i, t in enumerate(tiles):
                if i + 2 < n:
                    xs[tiles[i + 2]] = wo_load(tiles[i + 2])
                    do_mm(tiles[i + 2])
                if i + 1 < n:
                    stage1(tiles[i + 1])
                xt, xtb, slot = xs[t]
                st, stb = sts[t]
                xns[t] = norm_A2(xt, xtb, st, stb)
                if i >= 1:
                    tp = tiles[i - 1]
                    norm_B(xns[tp][0], xns[tp][1], hv, [c.hT[tp]], tp * 128)
                yield
            tp = tiles[-1]
            norm_B(xns[tp][0], xns[tp][1], hv, [c.hT[tp]], tp * 128)
            for w_ in wo_c:
                wfree(w_)

        def ffn_start(c):
            ainit = retire(list(cd["y"].values())) + retire(list(cd["kq"].values())) + retire(list(cd["aT"].values())) + retire([cd["memT"]])
            cd["aT"] = {(f, T2): Buf(init=ainit) for f in range(NF) for T2 in range(2)}

        def ld_out(half_, rg):
            kcn = 8 if rg < 2 else 6
            return wload(wfo[rg * 1024:rg * 1024 + kcn * 128, half_ * 512:(half_ + 1) * 512], kcn=kcn)

        def gen_ffn_in(c, hb):
            hv = c.H.fm()
            tok0 = hb * 1024
            ffn_start(c)
            aT_t = cd["aT"]
            order = (5, 0, 1, 2, 3, 4)

            def ld(ci):
                ncols_ = (4 if ci < 5 else 2) * 128
                return (wload(wfi[:, ci * 512:ci * 512 + ncols_], ncols=ncols_),
                        wload(wfi[:, DFF + ci * 512:DFF + ci * 512 + ncols_], ncols=ncols_))

            nxt_p = None
            for oi, ci in enumerate(order):
                nfl = 4 if ci < 5 else 2
                ncols = nfl * 128
                wg, wu = nxt_p if nxt_p is not None else ld(ci)
                nxt_p = None
                if oi + 1 < len(order) and wnfree() >= 2:
                    nxt_p = ld(order[oi + 1])
                elif oi + 1 == len(order):
                    c.pre_out = {}
                    for rg in range(3):
                        if wnfree() >= 1:
                            c.pre_out[(0, rg)] = ld_out(0, rg)
                for fl in range(nfl):
                    f = 4 * ci + fl
                    for T2 in range(2):
                        tc = slice(tok0 + T2 * 512, tok0 + (T2 + 1) * 512)
                        t0_ = (tok0 // 128) + 4 * T2
                        hts = [c.hT[t] for t in range(t0_, t0_ + 4)]
                        gbk, gbb = ringP.get()
                        mm(gbk[:], [(wg[0][:, kc, fl * 128:(fl + 1) * 128], hv[:, kc, tc]) for kc in range(KC)],
                           [gbb], [wg[1]] + hts)
                        sg, sgb = tring.get()
                        act(sg[:], gbk[:], AF.Silu, [sgb], [gbb])
                        ubk, ubb = ringP.get()
                        mm(ubk[:], [(wu[0][:, kc, fl * 128:(fl + 1) * 128], hv[:, kc, tc]) for kc in range(KC)],
                           [ubb], [wu[1]] + hts)
                        DVE.do(lambda e, ubk=ubk, sg=sg, f=f, T2=T2: e.tensor_tensor(
                            out=aT_v[:, f, T2 * 512:(T2 + 1) * 512], in0=ubk[:], in1=sg[:], op=ALU.mult),
                            outs=[aT_t[(f, T2)]], ins=[ubb, sgb])
                        yield
                wfree(wg)
                wfree(wu)

        def gen_ffn_out(c, hb):
            b = c.b
            aT_t = cd["aT"]
            pre_out = getattr(c, "pre_out", None) or {}
            c.pre_out = None
            for half in range(2):
                wch = [pre_out.pop((half, rg)) if (half, rg) in pre_out else ld_out(half, rg) for rg in range(3)]
                if half == 0:
                    for rg in range(2):
                        if wnfree() >= 1:
                            pre_out[(1, rg)] = ld_out(1, rg)
                    s_x = wfo[2048:2048 + 6 * 128, 512:1024].rearrange("(kc p) n -> p kc n", p=128)
                    POOL.dma(lambda e, s_x=s_x: e.dma_start(out=wx_v[:, 0:6, :], in_=s_x), wx_slot,
                             outs=[wx_b], extra=xak_b.all_toks() + xav_b.all_toks())
                    pre_out[(1, 2)] = (wx_v, wx_b, -1)

                def f_load(tl, half=half):
                    t = hb * 8 + tl
                    r0 = c.xrow0 + t * 128
                    xt, xtb, slot = xring.get()
                    SP.dma(lambda e, xt=xt, r0=r0, half=half: e.dma_start(
                        out=xt[:, 0:512], in_=out_d[r0:r0 + 128, half * 512:(half + 1) * 512]), slot,
                        outs=[xtb], ins=[x1d_t[(b, t)]])
                    return xt, xtb, slot

                nxt_x = f_load(0)
                for tl in range(8):
                    t = hb * 8 + tl
                    r0 = c.xrow0 + t * 128
                    xt, xtb, slot = nxt_x
                    if tl + 1 < 8:
                        nxt_x = f_load(tl + 1)
                    pbk, pbb = ringP.get()
                    for rg in range(3):
                        fs = range(8 * rg, min(8 * rg + 8, NF))
                        mm(pbk[:], [(aT_v[:, f, tl * 128:(tl + 1) * 128], wch[rg][0][:, f % 8, :]) for f in fs],
                           [pbb], [wch[rg][1]] + [aT_t[(f, tl // 4)] for f in fs], start=(rg == 0), stop=(rg == 2))
                    DVE.do(lambda e, xt=xt, pbk=pbk: e.tensor_tensor(
                        out=xt[:, 0:512], in0=pbk[:], in1=xt[:, 0:512], op=ALU.add), outs=[xtb], ins=[pbb])
                    ot = SP.dma(lambda e, xt=xt, r0=r0, half=half: e.dma_start(
                        out=out_d[r0:r0 + 128, half * 512:(half + 1) * 512], in_=xt[:, 0:512]), slot,
                        outs=[x1d_t[(b, t)]], ins=[xtb])
                    out_toks.append(ot)
                    yield
                for w_ in wch:
                    if w_[2] >= 0:
                        wfree(w_)

        def xa_start(c):
            kqi = retire(list(cd["kq"].values())) + retire([cd["memT"]])
            cd["kq"] = {(s_, T): Buf(init=kqi) for s_ in range(4) for T in range(4)}

        ctxs = [make_ctx(b) for b in range(NB)]
        for b in range(NB):
            c = ctxs[b]
            if b == 0:
                n1 = gen_N1(c)
                next(n1)
                next(n1)
                next(n1)
                next(n1)
                fv0 = gen_FV(c)
                next(fv0)
                late_consts()
                interleave(n1, fv0, every=1)
                late_consts_finish()
            else:
                run(gen_FV(c))
            run(gen_FV_final(c))
            fox_start(c)
            for h in range(8):
                run(gen_fox_units(c, h))
                run(gen_fox_attn(c, h))
            interleave(gen_merge(c, "fox", OFF_GB, True), gen_M(c), every=3)
            run(gen_conv(c))
            run(gen_merge(c, "conv", OFF_GA, False))
            xa_start(c)
            for hx in range(4):
                run(gen_xa_units(c, hx))
                run(gen_xa_attn(c, hx))
            run(gen_merge(c, "xa", OFF_GC, False))
            run(gen_wo(c, range(NT)))
            nxt_n1 = gen_N1(ctxs[b + 1]) if b + 1 < NB else None
            interleave(chain(gen_ffn_in(c, 0), gen_ffn_out(c, 0), gen_ffn_in(c, 1), gen_ffn_out(c, 1)), nxt_n1, every=6)


        SP.wait(out_toks)
        for it in wring.items:
            POOL.wait(it[1].all_toks())
        POOL.wait(wf_b.all_toks())

        with nc.Block() as block:
            @block.sync
            def _(e):
                SP.replay(e)

            @block.scalar
            def _(e):
                ACT.replay(e)

            @block.vector
            def _(e):
                DVE.replay(e)

            @block.tensor
            def _(e):
                PE.replay(e)

            @block.gpsimd
            def _(e):
                POOL.replay(e)
    return nc


_NC_CACHE = {}


def _consts():
    ident = np.eye(128, dtype=np.float32)
    s = np.arange(128)
    utri = (s[:, None] <= s[None, :]).astype(np.float32)
    j = s // 8
    h = s % 8
    same_h = h[:, None] == h[None, :]
    mjj = (same_h & (j[:, None] < j[None, :])).astype(np.float32)
    mjji = (same_h & (j[:, None] <= j[None, :])).astype(np.float32)
    mneg = np.where(s[:, None] > s[None, :], NEG, 0.0).astype(np.float32)
    return {"c_ident": ident, "c_utri": utri, "c_mjj": mjj, "c_mjji": mjji, "c_mneg": mneg}


def kernel(**inputs):
    if "nc" not in _NC_CACHE:
        _NC_CACHE["nc"] = build_program()
    nc = _NC_CACHE["nc"]
    f32 = lambda a: np.ascontiguousarray(np.asarray(a, dtype=np.float32))
    x = f32(inputs["x"])
    mem = f32(inputs["mem"])
    shared = {
        "norm1_g": f32(inputs["norm1_g"]).reshape(1, D),
        "w_in": f32(inputs["w_in"]).reshape(D, IN_COLS),
        "conv_w": f32(inputs["conv_w"]).reshape(3, D),
        "conv_b": f32(inputs["conv_b"]).reshape(1, D),
        "fox_f_bias": f32(inputs["fox_f_bias"]).reshape(1, 8),
        "fox_q_g": f32(inputs["fox_q_g"]).reshape(1, 128),
        "fox_k_g": f32(inputs["fox_k_g"]).reshape(1, 128),
        "mem_norm_g": f32(inputs["mem_norm_g"]).reshape(1, D),
        "w_mem_kv": f32(inputs["w_mem_kv"]).reshape(D, 2 * D),
        "xa_q_g": f32(inputs["xa_q_g"]).reshape(1, 256),
        "xa_k_g": f32(inputs["xa_k_g"]).reshape(1, 256),
        "w_br_conv": f32(inputs["w_br_conv"]).reshape(D, D),
        "w_br_fox": f32(inputs["w_br_fox"]).reshape(D, D),
        "w_br_xa": f32(inputs["w_br_xa"]).reshape(D, D),
        "w_o": f32(inputs["w_o"]).reshape(D, D),
        "norm2_g": f32(inputs["norm2_g"]).reshape(1, D),
        "w_ffn_in": f32(inputs["w_ffn_in"]).reshape(D, 2 * DFF),
        "w_ffn_out": f32(inputs["w_ffn_out"]).reshape(DFF, D),
    }
    shared.update(_consts())
    in_maps = []
    for c in range(N_CORES):
        m = dict(shared)
        m["x"] = np.ascontiguousarray(x[NB * c:NB * (c + 1)].reshape(NB * S, D))
        m["mem"] = np.ascontiguousarray(mem[NB * c:NB * (c + 1)].reshape(NB * NMEM, D))
        in_maps.append(m)
    res = run_bass_kernel_spmd(nc, in_maps, core_ids=list(range(N_CORES)))
    out = np.concatenate([np.asarray(r["out"]).reshape(NB, S, D) for r in res.results], axis=0)
    return out.astype(np.float32)
```

```python
import math
from contextlib import ExitStack

import numpy as np
import concourse.bass as bass
import concourse.mybir as mybir
from concourse.bass_utils import run_bass_kernel_spmd

F32 = mybir.dt.float32
BF16 = mybir.dt.bfloat16
AF = mybir.ActivationFunctionType
ALU = mybir.AluOpType

N_CORES = 8
D = 1024
S = 2048
NB = 2
NT = S // 128
KC = D // 128
NMEM = 256
DFF = 2816
NF = DFF // 128
IN_COLS = 10248
OFF_CB, OFF_CC, OFF_CV, OFF_FQ, OFF_FK, OFF_FV, OFF_XQ, OFF_GA, OFF_GB, OFF_GC, OFF_FF = (
    0, 1024, 2048, 3072, 4096, 5120, 6144, 7168, 8192, 9216, 10240)
EPS = 1e-6
NEG = -30000.0


class Tok:
    __slots__ = ("sem", "val")

    def __init__(self, sem, val):
        self.sem = sem
        self.val = val


class Buf:
    __slots__ = ("w", "r")

    def __init__(self, init=()):
        self.w = None
        self.r = {}
        for t in init:
            self.add_r(t)

    def add_r(self, t):
        if t is None:
            return
        k = id(t.sem)
        o = self.r.get(k)
        if o is None or o.val < t.val:
            self.r[k] = t

    def all_toks(self):
        return ([self.w] if self.w is not None else []) + list(self.r.values())


def retire(bufs):
    out = Buf()
    for b in bufs:
        for t in b.all_toks():
            out.add_r(t)
    return list(out.r.values())


class Eng:
    def __init__(self, name, sem, skip_self=False):
        self.name = name
        self.sem = sem
        self.cnt = 0
        self.ops = []
        self.waited = {}
        self.skip_self = skip_self

    def wait(self, toks):
        for t in toks:
            if t is None:
                continue
            if self.skip_self and t.sem is self.sem:
                continue
            k = id(t.sem)
            if self.waited.get(k, 0) >= t.val:
                continue
            self.waited[k] = t.val
            self.ops.append(lambda e, t=t: e.wait_ge(t.sem, t.val))

    @staticmethod
    def _deps(outs, ins, extra):
        deps = list(extra)
        for b in ins:
            deps.append(b.w)
        for b in outs:
            deps.extend(b.all_toks())
        return deps

    @staticmethod
    def _reg(tok, outs, ins):
        for b in ins:
            b.add_r(tok)
        for b in outs:
            b.w = tok
            b.r = {}

    def do(self, fn, outs=(), ins=(), extra=()):
        self.wait(self._deps(outs, ins, extra))
        self.cnt += 1
        tok = Tok(self.sem, self.cnt)
        sem = self.sem
        self.ops.append(lambda e: fn(e).then_inc(sem, 1))
        self._reg(tok, outs, ins)
        return tok

    def group(self, fns, outs=(), ins=(), extra=()):
        self.wait(self._deps(outs, ins, extra))
        for fn in fns[:-1]:
            self.ops.append(lambda e, fn=fn: fn(e))
        self.cnt += 1
        tok = Tok(self.sem, self.cnt)
        sem = self.sem
        last = fns[-1]
        self.ops.append(lambda e: last(e).then_inc(sem, 1))
        self._reg(tok, outs, ins)
        return tok

    def dma(self, fn, slot, outs=(), ins=(), extra=()):
        self.wait(self._deps(outs, ins, extra))
        slot[1] += 16
        tok = Tok(slot[0], slot[1])
        s = slot[0]
        self.ops.append(lambda e: fn(e).then_inc(s, 16))
        self._reg(tok, outs, ins)
        return tok

    def replay(self, e):
        for o in self.ops:
            o(e)


class Ring:
    def __init__(self, items):
        self.items = items
        self.i = 0

    def get(self):
        it = self.items[self.i % len(self.items)]
        self.i += 1
        return it


def build_program():
    nc = bass.Bass("TRN2", target_bir_lowering=False)

    def din(name, shape):
        return nc.dram_tensor(name, shape, F32, kind="ExternalInput").ap()

    x_d = din("x", [NB * S, D])
    mem_d = din("mem", [NB * NMEM, D])
    n1g_d = din("norm1_g", [1, D])
    w_in = din("w_in", [D, IN_COLS])
    conv_w_d = din("conv_w", [3, D])
    conv_b_d = din("conv_b", [1, D])
    fbias_d = din("fox_f_bias", [1, 8])
    fqg_d = din("fox_q_g", [1, 128])
    fkg_d = din("fox_k_g", [1, 128])
    mng_d = din("mem_norm_g", [1, D])
    wmkv = din("w_mem_kv", [D, 2 * D])
    xqg_d = din("xa_q_g", [1, 256])
    xkg_d = din("xa_k_g", [1, 256])
    wbr = {"conv": din("w_br_conv", [D, D]), "fox": din("w_br_fox", [D, D]), "xa": din("w_br_xa", [D, D])}
    wo_d = din("w_o", [D, D])
    n2g_d = din("norm2_g", [1, D])
    wfi = din("w_ffn_in", [D, 2 * DFF])
    wfo = din("w_ffn_out", [DFF, D])
    c_ident = din("c_ident", [128, 128])
    c_utri = din("c_utri", [128, 128])
    c_mjj = din("c_mjj", [128, 128])
    c_mjji = din("c_mjji", [128, 128])
    c_mneg = din("c_mneg", [128, 128])
    out_d = nc.dram_tensor("out", [NB * S, D], F32, kind="ExternalOutput").ap()

    with ExitStack() as es:
        def sb(name, shape, dt):
            return es.enter_context(nc.sbuf_tensor(name, shape, dt))

        def ps(name, shape, dt):
            return es.enter_context(nc.psum_tensor(name, shape, dt))

        def sem(name):
            return es.enter_context(nc.semaphore(name))

        PE = Eng("pe", sem("s_pe"), skip_self=True)
        ACT = Eng("act", sem("s_act"))
        DVE = Eng("dve", sem("s_dve"))
        POOL = Eng("pool", sem("s_pool"))
        SP = Eng("sp", sem("s_sp"))

        A = sb("A", [128, 16384], BF16)
        Bt = sb("B", [128, 16384], BF16)
        CD = sb("CD", [128, 24576], BF16)
        y_v = CD[:, 8192:24576].rearrange("p (a b) -> p a b", a=KC)
        aT_v = CD[:, 0:NF * 1024].rearrange("p (a b) -> p a b", a=NF)
        memT_v = CD[:, 0:2048].rearrange("p (a b) -> p a b", a=KC)

        NW = 5
        wbufs = [sb(f"w{i}", [128, KC, 512], BF16) for i in range(NW)]
        wring = Ring([(wbufs[i], Buf(), [sem(f"dw{i}"), 0]) for i in range(NW)])

        xts = [sb(f"xt{i}", [128, D], F32) for i in range(3)]
        xring = Ring([(xts[i], Buf(), [sem(f"dx{i}"), 0]) for i in range(3)])
        gt = sb("gt", [128, D], F32)
        gt_b = Buf()
        gt_slot = [sem("dgt"), 0]
        xnbs = [sb(f"xnb{i}", [128, D], BF16) for i in range(2)]
        xnring = Ring([(xnbs[i], Buf()) for i in range(2)])
        NTMP = 5
        tmps = [sb(f"tmp{i}", [128, 512], F32) for i in range(NTMP)]
        tring = Ring([(tmps[i], Buf()) for i in range(NTMP)])
        pbs = [sb(f"pb{i}", [128, 512], BF16) for i in range(4)]
        pring = Ring([(pbs[i], Buf()) for i in range(4)])
        sqs = [sb(f"sq{i}", [128, 512], BF16) for i in range(3)]
        sqring = Ring([(sqs[i], Buf()) for i in range(3)])
        us = [sb(f"u{i}", [128, 514], F32) for i in range(2)]
        uring = Ring([(us[i], Buf()) for i in range(2)])
        st_small = [sb(f"st{i}", [128, 4], F32) for i in range(4)]
        string = Ring([(st_small[i], Buf()) for i in range(4)])

        ident32 = sb("ident32", [128, 128], F32)
        utri32 = sb("utri32", [128, 128], F32)
        mjj32 = sb("mjj32", [128, 128], F32)
        mjji32 = sb("mjji32", [128, 128], F32)
        ones32 = sb("ones32", [128, 128], F32)
        ident_bf = sb("ident_bf", [128, 128], BF16)
        mneg_bf = sb("mneg_bf", [128, 128], BF16)
        ones_bf = sb("ones_bf", [128, 128], BF16)
        rows = sb("rows", [38, 128], F32)
        cols = sb("cols", [128, 38], F32)
        fb_b = sb("fb_b", [128, NT, 8], F32)
        xakv = sb("xakv", [128, 4096], BF16)
        xa_kT = xakv[:, 0:2048].rearrange("p (a b) -> p a b", a=KC)
        xa_V = xakv[:, 2048:4096].rearrange("p (a b) -> p a b", a=2)
        wx_v = xakv[:].rearrange("p (a b) -> p a b", a=KC)
        wx_b = Buf()
        wx_slot = [sem("dwx"), 0]
        bias_all = sb("bias_all", [128, 40, 8], F32)
        wf_sb = sb("wf_sb", [128, KC, 8], BF16)
        const_b = Buf()
        cols_b = Buf()
        xak_b = Buf()
        xav_b = Buf()
        bias_b = Buf()
        wf_b = Buf()
        wf_slot = [sem("dwf"), 0]

        banks = [ps(f"bank{i}", [128, 512], F32) for i in range(7)]
        bank_tr = ps("bank_tr", [128, KC, 128], BF16)
        bk = [(banks[i], Buf()) for i in range(7)]
        tr_b = Buf()
        ringP = Ring(bk[0:4])
        ringQ = Ring(bk[4:7])
        ringS = Ring(bk[0:3])
        ringO = Ring(bk[3:5])
        ringL = Ring(bk[5:7])

        hT_t = [Buf() for _ in range(NT)]
        B_V_t = [Buf() for _ in range(NT)]
        B_M_t = {}
        kq_t = [Buf() for _ in range(4)]
        y_t = {(f, T): Buf() for f in range(KC) for T in range(4)}
        memT_b = Buf()
        aT_t = {}
        x1d_t = {}
        out_toks = []

        wheld = [False] * NW
        wpos = [0]

        wfreed_at = [0] * NW
        wclock = [0]

        def wnfree():
            return sum(1 for h_ in wheld if not h_)

        def wget():
            cands = [i for i in range(NW) if not wheld[i]]
            if not cands:
                raise RuntimeError("no free weight buffer")
            i = min(cands, key=lambda j: wfreed_at[j])
            wheld[i] = True
            wb, b, slot = wring.items[i]
            return wb, b, slot, i

        def wfree(w):
            assert wheld[w[2]]
            wheld[w[2]] = False
            wclock[0] += 1
            wfreed_at[w[2]] = wclock[0]

        def wload(src, kcn=KC, ncols=512):
            wb, b, slot, i = wget()
            dst = wb[:, 0:kcn, 0:ncols]
            s = src.rearrange("(kc p) n -> p kc n", p=128)
            POOL.dma(lambda e: e.dma_start(out=dst, in_=s), slot, outs=[b])
            return wb, b, i

        def wload3(c0s, ncols=128):
            wb, b, slot, i = wget()
            first = True
            for j, c0 in enumerate(c0s):
                dst = wb[:, :, j * ncols:(j + 1) * ncols]
                s = w_in[:, c0:c0 + ncols].rearrange("(kc p) n -> p kc n", p=128)
                if first:
                    POOL.dma(lambda e, dst=dst, s=s: e.dma_start(out=dst, in_=s), slot, outs=[b])
                    first = False
                else:
                    slot[1] += 16
                    tok = Tok(slot[0], slot[1])
                    sl = slot[0]
                    POOL.ops.append(lambda e, dst=dst, s=s: e.dma_start(out=dst, in_=s).then_inc(sl, 16))
                    b.w = tok
            return wb, b, i

        def mm(out_ap, pairs, outs, ins, start=True, stop=True):
            n = len(pairs)
            fns = []
            for i, (l, r) in enumerate(pairs):
                fns.append(lambda e, l=l, r=r, i=i: e.matmul(out_ap, lhsT=l, rhs=r,
                                                             start=(start and i == 0), stop=(stop and i == n - 1)))
            return PE.group(fns, outs=outs, ins=ins)

        def act(out, in_, func, outs, ins, **kw):
            return ACT.do(lambda e: e.activation(out=out, in_=in_, func=func, **kw), outs=outs, ins=ins)

        def rstd_small(ssq_ap, ssq_b, n):
            st, stb = string.get()
            act(st[:, 0:1], ssq_ap, AF.Ln, [stb], [ssq_b], scale=1.0 / n, bias=EPS)
            act(st[:, 1:2], st[:, 0:1], AF.Exp, [stb], [stb], scale=-0.5)
            return st[:, 1:2], stb

        def load_gain(src):
            SP.dma(lambda e: e.dma_start(out=gt[:], in_=src.partition_broadcast(128)), gt_slot, outs=[gt_b])

        def norm_tile(xt, xtb, dstT, dst_bufs, ncol0):
            xn, xnb = norm_A(xt, xtb)
            norm_B(xn, xnb, dstT, dst_bufs, ncol0)

        junk_ap = us[0][:].bitcast(BF16)[:, 0:D]
        junk_b = uring.items[0][1]

        def norm_A1(xt, xtb):
            st, stb = string.get()
            act(junk_ap, xt[:], AF.Square, [junk_b, stb], [xtb], accum_out=st[:, 2:3])
            act(st[:, 0:1], st[:, 2:3], AF.Ln, [stb], [stb], scale=1.0 / D, bias=EPS)
            act(st[:, 1:2], st[:, 0:1], AF.Exp, [stb], [stb], scale=-0.5)
            return st, stb

        def norm_A2(xt, xtb, st, stb):
            xn, xnb = xnring.get()
            DVE.do(lambda e: e.scalar_tensor_tensor(out=xn[:], in0=xt[:], scalar=st[:, 1:2], in1=gt[:],
                                                    op0=ALU.mult, op1=ALU.mult),
                   outs=[xnb], ins=[xtb, stb, gt_b])
            return xn, xnb

        def norm_A(xt, xtb):
            st, stb = norm_A1(xt, xtb)
            return norm_A2(xt, xtb, st, stb)

        def norm_B(xn, xnb, dstT, dst_bufs, ncol0):
            fns = [lambda e, kc=kc: e.transpose(out=bank_tr[:, kc, :], in_=xn[:, kc * 128:(kc + 1) * 128],
                                                identity=ident_bf[:]) for kc in range(KC)]
            PE.group(fns, outs=[tr_b], ins=[xnb, const_b])
            act(dstT[:, :, ncol0:ncol0 + 128], bank_tr[:], AF.Copy, dst_bufs, [tr_b])

        cslotA = [sem("dconstA"), 0]
        cslotB = [sem("dconstB"), 0]
        const2_b = Buf()
        rows_b = const2_b
        SP.dma(lambda e: e.dma_start(out=ident32[:], in_=c_ident), cslotA, outs=[const_b])
        mstage, mstage_b = tring.get()
        cdmas = [(utri32[:], c_utri), (mjj32[:], c_mjj), (mjji32[:], c_mjji),
                 (mstage[:, 0:128], c_mneg),
                 (rows[0:24, :], conv_w_d.rearrange("k (f p) -> (k f) p", p=128)),
                 (rows[24:32, :], conv_b_d.rearrange("o (f p) -> (o f) p", p=128)),
                 (rows[32:33, :], fqg_d), (rows[33:34, :], fkg_d),
                 (rows[34:36, :], xqg_d.rearrange("o (c p) -> (o c) p", p=128)),
                 (rows[36:38, :], xkg_d.rearrange("o (c p) -> (o c) p", p=128)),
                 (fb_b[:], bass.AP(fbias_d.tensor, 0, [[0, 128], [0, NT], [1, 8]]))]
        deferred_const = []
        for dst, src in cdmas:
            cslotB[1] += 16
            slB = cslotB[0]
            deferred_const.append(lambda e, dst=dst, src=src: e.dma_start(out=dst, in_=src).then_inc(slB, 16))
        const2_b.w = Tok(cslotB[0], cslotB[1])
        mstage_b.w = const2_b.w
        DVE.do(lambda e: e.tensor_copy(out=ident_bf[:], in_=ident32[:]), outs=[const_b], ins=[const_b])
        DVE.do(lambda e: e.memset(ones32[:], 1.0), outs=[const_b], ins=[const_b])
        DVE.do(lambda e: e.memset(ones_bf[:], 1.0), outs=[const_b], ins=[const_b])
        DVE.do(lambda e: e.memset(bias_all[:], 0.0), outs=[bias_b], ins=[])

        def late_consts():
            for fn in deferred_const:
                POOL.ops.append(fn)

        def late_consts_finish():
            DVE.do(lambda e: e.tensor_copy(out=mneg_bf[:], in_=mstage[:, 0:128]), outs=[const2_b], ins=[const2_b, mstage_b])
            pb0, pb0b = ringQ.get()
            mm(pb0[:, 0:38], [(rows[0:38, :], ident32[0:38, 0:38])], [pb0b], [rows_b, const_b])
            DVE.do(lambda e: e.tensor_copy(out=cols[:], in_=pb0[:, 0:38]), outs=[cols_b], ins=[pb0b])

        def col(i):
            return cols[:, i:i + 1]

        class Region:
            def __init__(self, t):
                self.t = t
                self.trk = {}

            def view(self, keys):
                init = retire(list(self.trk.values()))
                self.trk = {k: Buf(init=init) for k in keys}
                return self.trk

            def fm(self):
                return self.t[:].rearrange("p (a b) -> p a b", a=KC)

            def tm(self):
                return self.t[:].rearrange("p (a b) -> p a b", a=NT)

        regs = [Region(A), Region(Bt)]
        cd = {"kq": {(s_, T): Buf() for s_ in range(4) for T in range(4)}, "memT": Buf(),
              "y": {(f, T): Buf() for f in range(KC) for T in range(4)}, "aT": {}}

        class Ctx:
            pass

        def make_ctx(b):
            c = Ctx()
            c.b = b
            c.xrow0 = b * S
            c.H = regs[b % 2]
            c.G = regs[(b + 1) % 2]
            c.hT = None
            c.Vt = None
            c.Mt = None
            c.bidx = {}
            return c

        def run(gen):
            if gen is None:
                return
            for _ in gen:
                pass

        def interleave(main, side, every, lag=0):
            n = 0
            side_live = side is not None
            for _ in main:
                n += 1
                if side_live and n > lag and (n - lag) % every == 0:
                    try:
                        next(side)
                    except StopIteration:
                        side_live = False
            if side_live:
                for _ in side:
                    pass

        def chain(*gens):
            for g in gens:
                if g is None:
                    continue
                for x in g:
                    yield x

        def gen_M(c):
            b = c.b
            load_gain(mng_d)
            cd["memT"] = Buf(init=retire(list(cd["kq"].values())) + retire(list(cd["aT"].values())))
            memT_b = cd["memT"]
            for t_ in wx_b.all_toks():
                xak_b.add_r(t_)
                xav_b.add_r(t_)
            wcur = wload(wmkv[:, 0:512])
            prevB = None
            for mt in range(2):
                xt, xtb, slot = xring.get()
                r0 = b * NMEM + mt * 128
                SP.dma(lambda e, xt=xt, r0=r0: e.dma_start(out=xt[:], in_=mem_d[r0:r0 + 128, :]), slot, outs=[xtb])
                xn, xnb = norm_A(xt, xtb)
                if prevB is not None:
                    norm_B(prevB[0], prevB[1], memT_v, [memT_b], prevB[2] * 128)
                prevB = (xn, xnb, mt)
                yield
            norm_B(prevB[0], prevB[1], memT_v, [memT_b], prevB[2] * 128)
            yield
            rP, rQ = Ring(bk[4:6]), Ring(bk[6:7])
            for hx in range(4):
                pcs = []
                sqc = []
                for c_ in range(2):
                    f = 2 * hx + c_
                    wb, wbb = wcur[0:2]
                    pbk, pbb = rP.get()
                    mm(pbk[:, 0:NMEM], [(wb[:, kc, (f % 4) * 128:(f % 4 + 1) * 128], memT_v[:, kc, :]) for kc in range(KC)],
                       [pbb], [wbb, memT_b])
                    sq, sqb = sqring.get()
                    act(sq[:, 0:NMEM], pbk[:, 0:NMEM], AF.Square, [sqb], [pbb])
                    pcs.append((pbk, pbb))
                    sqc.append((sq, sqb))
                qb_, qbb = rQ.get()
                mm(qb_[:, 0:NMEM], [(ones_bf[:], sqc[c_][0][:, 0:NMEM]) for c_ in range(2)], [qbb],
                   [sqc[0][1], sqc[1][1], const_b])
                t1, t1b = tring.get()
                act(t1[:, 0:NMEM], qb_[:, 0:NMEM], AF.Ln, [t1b], [qbb], scale=1.0 / 256, bias=EPS)
                act(t1[:, 0:NMEM], t1[:, 0:NMEM], AF.Exp, [t1b], [t1b], scale=-0.5)
                for c_ in range(2):
                    f = 2 * hx + c_
                    pbk, pbb = pcs[c_]
                    DVE.do(lambda e, pbk=pbk, f=f, c_=c_, t1=t1: e.scalar_tensor_tensor(
                        out=xa_kT[:, f, :], in0=pbk[:, 0:NMEM], scalar=col(36 + c_), in1=t1[:, 0:NMEM],
                        op0=ALU.mult, op1=ALU.mult), outs=[xak_b], ins=[pbb, t1b, cols_b])
                if hx == 1:
                    wfree(wcur)
                    wcur = wload(wmkv[:, 512:1024])
                if hx == 3:
                    wfree(wcur)
                    wcur = wload(wmkv[:, D:D + 512])
                yield
            for half in range(2):
                for mt in range(2):
                    wb, wbb = wcur[0:2]
                    pbk, pbb = rP.get()
                    mm(pbk[:], [(memT_v[:, kc, mt * 128:(mt + 1) * 128], wb[:, kc, :]) for kc in range(KC)],
                       [pbb], [wbb, memT_b])
                    act(xa_V[:, mt, half * 512:(half + 1) * 512], pbk[:], AF.Copy, [xav_b], [pbb])
                    if half == 0 and mt == 1:
                        wfree(wcur)
                        wcur = wload(wmkv[:, D + 512:D + 1024])
                    yield
            wfree(wcur)

        def gen_N1(c):
            load_gain(n1g_d)
            c.hT = c.H.view(range(NT))
            hv = c.H.fm()
            prevB = None
            for t in range(NT):
                xt, xtb, slot = xring.get()
                r0 = c.xrow0 + t * 128
                SP.dma(lambda e, xt=xt, r0=r0: e.dma_start(out=xt[:], in_=x_d[r0:r0 + 128, :]), slot, outs=[xtb])
                xn, xnb = norm_A(xt, xtb)
                if prevB is not None:
                    norm_B(prevB[0], prevB[1], hv, [c.hT[prevB[2]]], prevB[2] * 128)
                prevB = (xn, xnb, t)
                yield
            norm_B(prevB[0], prevB[1], hv, [c.hT[prevB[2]]], prevB[2] * 128)
            yield

        def hT_T(c, T):
            return [c.hT[t] for t in range(4 * T, 4 * T + 4)]

        def gen_FV(c):
            c.Vt = c.G.view(range(NT))
            hv = c.H.fm()
            Vv = c.G.tm()
            wv2 = getattr(c, "pre_fv", None) or [wload(w_in[:, OFF_FV + cc_ * 512:OFF_FV + (cc_ + 1) * 512]) for cc_ in range(2)]
            c.pre_fv = None
            s_wf = w_in[:, OFF_FF:OFF_FF + 8].rearrange("(kc p) n -> p kc n", p=128)
            POOL.dma(lambda e: e.dma_start(out=wf_sb[:], in_=s_wf), wf_slot, outs=[wf_b])
            fbk, fbb = ringQ.get()
            c.fbk = (fbk, fbb)
            for t in range(NT):
                for half in range(2):
                    wb, wbb = wv2[half][0:2]
                    pbk, pbb = ringP.get()
                    mm(pbk[:], [(hv[:, kc, t * 128:(t + 1) * 128], wb[:, kc, :]) for kc in range(KC)],
                       [pbb], [wbb, c.hT[t]])
                    if half == 0:
                        act(Vv[:, t, half * 512:(half + 1) * 512], pbk[:], AF.Copy, [c.Vt[t]], [pbb])
                    else:
                        DVE.do(lambda e, t=t, half=half, pbk=pbk: e.tensor_copy(
                            out=Vv[:, t, half * 512:(half + 1) * 512], in_=pbk[:]), outs=[c.Vt[t]], ins=[pbb])
                mm(fbk[:, t * 8:(t + 1) * 8], [(hv[:, kc, t * 128:(t + 1) * 128], wf_sb[:, kc, :]) for kc in range(KC)],
                   [fbb], [wf_b, c.hT[t]])
                if t == NT - 1:
                    for w_ in wv2:
                        wfree(w_)
                    fox_w["k"] = wload(w_in[:, OFF_FK:OFF_FK + 512])
                    fox_w["q"] = wload(w_in[:, OFF_FQ:OFF_FQ + 512])
                yield

        def gen_FV_final(c):
            fbk, fbb = c.fbk
            fv, fvb = tring.get()
            lf_sb = fv[:, 0:128]
            z_sb = fv[:, 128:256]
            c_sb = fv[:, 256:384]
            cend_sb = fv[:, 384:512]
            DVE.do(lambda e: e.tensor_tensor(out=lf_sb, in0=fbk[:, 0:128], in1=fb_b[:].rearrange("p a b -> p (a b)"),
                                             op=ALU.add), outs=[fvb], ins=[fbb, const_b, const2_b])
            act(lf_sb, lf_sb, AF.Exp, [fvb], [fvb], scale=-1.0)
            act(lf_sb, lf_sb, AF.Ln, [fvb], [fvb], bias=1.0)
            zbk, zbb = ringQ.get()
            mm(zbk[:, 0:128], [(lf_sb, ones32[:])], [zbb], [fvb, const_b])
            act(z_sb, zbk[:, 0:128], AF.Copy, [fvb], [zbb])
            cbk, cbb = ringQ.get()
            mm(cbk[:, 0:128], [(utri32[:], lf_sb), (z_sb, mjj32[:])], [cbb], [fvb, const_b, const2_b])
            DVE.do(lambda e: e.tensor_copy(out=c_sb, in_=cbk[:, 0:128]), outs=[fvb], ins=[cbb])
            ebk, ebb = ringQ.get()
            mm(ebk[:, 0:128], [(z_sb, mjji32[:])], [ebb], [fvb, const_b, const2_b])
            DVE.do(lambda e: e.tensor_copy(out=cend_sb, in_=ebk[:, 0:128]), outs=[fvb], ins=[ebb])
            k = 0
            for qb in range(4):
                for j in range(4 * qb + 4):
                    c.bidx[(qb, j)] = k
                    jr = 4 * qb + 1
                    DVE.do(lambda e, k=k, j=j, jr=jr: e.tensor_tensor(
                        out=bias_all[:, k, :], in0=c_sb[:, j * 8:(j + 1) * 8], in1=cend_sb[:, jr * 8:(jr + 1) * 8],
                        op=ALU.subtract), outs=[bias_b], ins=[fvb])
                    k += 1
            yield

        scale_fox = 1.0 / math.sqrt(128.0)
        fox_w = {}
        bank_tr32 = bank_tr[:].rearrange("p a b -> p (a b)").bitcast(F32)
        fox_rings = (ringS, ringO, ringL)
        xa_rings = (Ring(bk[0:4]), Ring(bk[4:6]), Ring(bk[6:7]))

        def fox_start(c):
            kqi = retire([cd["memT"]]) + retire(list(cd["kq"].values()))
            cd["kq"] = {(s_, T): Buf(init=kqi) for s_ in range(4) for T in range(4)}
            y_init = retire(list(cd["aT"].values()))
            cd["y"] = {(f, T): Buf(init=retire([cd["y"][(f, T)]]) + y_init) for f in range(KC) for T in range(4)}

        def gen_fox_units(c, h):
            hv = c.H.fm()
            rP, rQ = Ring(bk[0:4]), Ring([(bank_tr32, tr_b)])
            if h == 4:
                g = h // 4
                for w_ in fox_w.values():
                    wfree(w_)
                fox_w["k"] = wload(w_in[:, OFF_FK + g * 512:OFF_FK + (g + 1) * 512])
                fox_w["q"] = wload(w_in[:, OFF_FQ + g * 512:OFF_FQ + (g + 1) * 512])
            wk_c, wq_c = fox_w["k"][0:2], fox_w["q"][0:2]
            set_ = h % 2
            kslot, qslot = 2 * set_, 2 * set_ + 1
            kT = CD[:, kslot * 2048:(kslot + 1) * 2048]
            qT = CD[:, qslot * 2048:(qslot + 1) * 2048]
            units = [(which, T) for T in range(4) for which in (0, 1)]
            pend = None
            hl = h % 4
            for u in range(len(units) + 1):
                cur = None
                if u < len(units):
                    which, T = units[u]
                    wb, wbb = (wk_c, wq_c)[which]
                    pbk, pbb = rP.get()
                    mm(pbk[:], [(wb[:, kc, hl * 128:(hl + 1) * 128], hv[:, kc, T * 512:(T + 1) * 512]) for kc in range(KC)],
                       [pbb], [wbb] + hT_T(c, T))
                    sq, sqb = sqring.get()
                    act(sq[:], pbk[:], AF.Square, [sqb], [pbb])
                    cur = (which, T, pbk, pbb, sq, sqb)
                if pend is not None:
                    which, T, pbk, pbb, sq, sqb = pend
                    qb_, qbb = rQ.get()
                    mm(qb_[:], [(ones_bf[:], sq[:])], [qbb], [sqb, const_b])
                    t1, t1b = tring.get()
                    act(t1[:], qb_[:], AF.Ln, [t1b], [qbb], scale=1.0 / 128, bias=EPS)
                    act(t1[:], t1[:], AF.Exp, [t1b], [t1b], scale=-0.5)
                    dst = (kT, qT)[which][:, T * 512:(T + 1) * 512]
                    dstb = cd["kq"][((kslot, qslot)[which], T)]
                    gcol = col(33) if which == 0 else col(32)
                    DVE.do(lambda e, dst=dst, pbk=pbk, gcol=gcol, t1=t1: e.scalar_tensor_tensor(
                        out=dst, in0=pbk[:], scalar=gcol, in1=t1[:], op0=ALU.mult, op1=ALU.mult),
                        outs=[dstb], ins=[pbb, t1b, cols_b])
                pend = cur
                yield
            if h == 7:
                for w_ in fox_w.values():
                    wfree(w_)
                fox_w.clear()

        def gen_fox_attn(c, h):
            Vv = c.G.tm()
            set_ = h % 2
            kslot, qslot = 2 * set_, 2 * set_ + 1
            kT = CD[:, kslot * 2048:(kslot + 1) * 2048]
            qT = CD[:, qslot * 2048:(qslot + 1) * 2048]
            rS, rO, rL = fox_rings
            blocks = [(qb, j) for qb in range(4) for j in range(4 * qb + 4)]

            def qk(i):
                qb, j = blocks[i]
                r = j - 4 * qb
                c0 = max(r, 0) * 128
                sbk, sbb = rS.get()
                l_ap = kT[:, j * 128:(j + 1) * 128]
                r_ap = qT[:, qb * 512 + c0:(qb + 1) * 512]
                fns = [lambda e, sbk=sbk, c0=c0, l_ap=l_ap, r_ap=r_ap, r=r: e.matmul(
                    sbk[:, c0:512], lhsT=l_ap, rhs=r_ap, start=True, stop=(r < 0))]
                if r >= 0:
                    fns.append(lambda e, sbk=sbk, c0=c0: e.matmul(
                        sbk[:, c0:c0 + 128], lhsT=ident_bf[:], rhs=mneg_bf[:], start=False, stop=True))
                PE.group(fns, outs=[sbb], ins=[cd["kq"][(kslot, j // 4)], cd["kq"][(qslot, qb)], const_b, const2_b])
                return sbk, sbb, c0

            pend_qk = [qk(0), qk(1)]
            obk = obb = lbk = lbb = None
            for i, (qb, j) in enumerate(blocks):
                nj = 4 * qb + 4
                if j == 0:
                    obk, obb = rO.get()
                    lbk, lbb = rL.get()
                sbk, sbb, c0 = pend_qk.pop(0)
                if i + 2 < len(blocks):
                    pend_qk.append(qk(i + 2))
                p, pbuf = pring.get()
                bi = c.bidx[(qb, j)]
                act(p[:, c0:512], sbk[:, c0:512], AF.Exp, [pbuf], [sbb, bias_b],
                    scale=scale_fox, bias=bias_all[:, bi, h:h + 1])
                fns = [lambda e, obk=obk, c0=c0, j=j, p=p, nj=nj, h=h: e.matmul(
                           obk[:, c0:512], lhsT=Vv[:, j, h * 128:(h + 1) * 128], rhs=p[:, c0:512],
                           start=(j == 0), stop=(j == nj - 1)),
                       lambda e, lbk=lbk, c0=c0, j=j, p=p, nj=nj: e.matmul(
                           lbk[:, c0:512], lhsT=ones_bf[:], rhs=p[:, c0:512],
                           start=(j == 0), stop=(j == nj - 1))]
                PE.group(fns, outs=[obb, lbb], ins=[pbuf, c.Vt[j], const_b])
                if j == nj - 1:
                    t1, t1b = tring.get()
                    DVE.do(lambda e, t1=t1, lbk=lbk: e.tensor_copy(out=t1[:], in_=lbk[:]), outs=[t1b], ins=[lbb])
                    DVE.do(lambda e, t1=t1: e.reciprocal(out=t1[:], in_=t1[:]), outs=[t1b], ins=[t1b])
                    DVE.do(lambda e, t1=t1, obk=obk, qb=qb, h=h: e.tensor_tensor(
                        out=y_v[:, h, qb * 512:(qb + 1) * 512], in0=obk[:], in1=t1[:], op=ALU.mult),
                        outs=[cd["y"][(h, qb)]], ins=[obb, t1b])
                yield

        def gen_merge(c, name, gate_off, first):
            w = wbr[name]
            hv = c.H.fm()
            Mv = c.G.fm()
            if first:
                c.Mt = c.G.view([(f, T) for f in range(KC) for T in range(4)])
            nxt_w = None
            for g in range(2):
                if nxt_w is not None:
                    wbp_, wgp_ = nxt_w
                    nxt_w = None
                else:
                    wbp_ = wload(w[:, g * 512:(g + 1) * 512])
                    wgp_ = wload(w_in[:, gate_off + g * 512:gate_off + (g + 1) * 512])
                if g == 0 and wnfree() >= 3:
                    nxt_w = (wload(w[:, 512:1024]), wload(w_in[:, gate_off + 512:gate_off + 1024]))
                wbp, wgp = wbp_[0:2], wgp_[0:2]
                for fl in range(4):
                    f = 4 * g + fl
                    for T in range(4):
                        gbk, gbb = ringP.get()
                        mm(gbk[:], [(wgp[0][:, kc, fl * 128:(fl + 1) * 128], hv[:, kc, T * 512:(T + 1) * 512])
                                    for kc in range(KC)], [gbb], [wgp[1]] + hT_T(c, T))
                        sg, sgb = tring.get()
                        act(sg[:], gbk[:], AF.Sigmoid, [sgb], [gbb])
                        pbk, pbb = ringP.get()
                        mm(pbk[:], [(wbp[0][:, kc, fl * 128:(fl + 1) * 128], y_v[:, kc, T * 512:(T + 1) * 512])
                                    for kc in range(KC)], [pbb], [wbp[1]] + [cd["y"][(kc, T)] for kc in range(KC)])
                        mdst = Mv[:, f, T * 512:(T + 1) * 512]
                        mb = c.Mt[(f, T)]
                        if first:
                            DVE.do(lambda e, mdst=mdst, pbk=pbk, sg=sg: e.tensor_tensor(
                                out=mdst, in0=pbk[:], in1=sg[:], op=ALU.mult), outs=[mb], ins=[pbb, sgb])
                        else:
                            DVE.do(lambda e, pbk=pbk, sg=sg: e.tensor_tensor(
                                out=sg[:], in0=pbk[:], in1=sg[:], op=ALU.mult), outs=[sgb], ins=[pbb])
                            DVE.do(lambda e, mdst=mdst, sg=sg: e.tensor_tensor(
                                out=mdst, in0=sg[:], in1=mdst, op=ALU.add), outs=[mb], ins=[sgb])
                        yield
                wfree(wbp_)
                wfree(wgp_)

        def gen_conv(c):
            hv = c.H.fm()
            nxt_wc = None
            for f in range(KC):
                wc_ = nxt_wc if nxt_wc is not None else wload3([OFF_CC + f * 128, OFF_CV + f * 128, OFF_CB + f * 128])
                nxt_wc = None
                if f + 1 < KC and wnfree() >= 1:
                    nxt_wc = wload3([OFF_CC + (f + 1) * 128, OFF_CV + (f + 1) * 128, OFF_CB + (f + 1) * 128])
                wb, wbb = wc_[0:2]
                uprev = None
                for T in range(4):
                    cols_T = slice(T * 512, (T + 1) * 512)
                    ccbk, ccbb = ringP.get()
                    mm(ccbk[:], [(wb[:, kc, 0:128], hv[:, kc, cols_T]) for kc in range(KC)], [ccbb], [wbb] + hT_T(c, T))
                    ccs, ccsb = tring.get()
                    act(ccs[:], ccbk[:], AF.Copy, [ccsb], [ccbb])
                    cvbk, cvbb = ringP.get()
                    mm(cvbk[:], [(wb[:, kc, 128:256], hv[:, kc, cols_T]) for kc in range(KC)], [cvbb], [wbb] + hT_T(c, T))
                    u, ub = uring.get()
                    if uprev is None:
                        DVE.do(lambda e, u=u: e.memset(u[:, 0:2], 0.0), outs=[ub], ins=[])
                    else:
                        up, upb = uprev
                        DVE.do(lambda e, u=u, up=up: e.tensor_copy(out=u[:, 0:2], in_=up[:, 512:514]), outs=[ub], ins=[upb])
                    DVE.do(lambda e, u=u, cvbk=cvbk, ccs=ccs: e.tensor_tensor(
                        out=u[:, 2:514], in0=cvbk[:], in1=ccs[:], op=ALU.mult), outs=[ub], ins=[cvbb, ccsb])
                    uprev = (u, ub)
                    ta, tab = tring.get()
                    DVE.do(lambda e, ta=ta, u=u, f=f: e.tensor_scalar(
                        out=ta[:], in0=u[:, 2:514], scalar1=col(16 + f), scalar2=col(24 + f),
                        op0=ALU.mult, op1=ALU.add), outs=[tab], ins=[ub, cols_b])
                    DVE.do(lambda e, ta=ta, u=u, f=f: e.scalar_tensor_tensor(
                        out=ta[:], in0=u[:, 1:513], scalar=col(8 + f), in1=ta[:], op0=ALU.mult, op1=ALU.add),
                        outs=[tab], ins=[ub, cols_b])
                    DVE.do(lambda e, ta=ta, u=u, f=f: e.scalar_tensor_tensor(
                        out=ta[:], in0=u[:, 0:512], scalar=col(0 + f), in1=ta[:], op0=ALU.mult, op1=ALU.add),
                        outs=[tab], ins=[ub, cols_b])
                    cbbk, cbbb = ringP.get()
                    mm(cbbk[:], [(wb[:, kc, 256:384], hv[:, kc, cols_T]) for kc in range(KC)], [cbbb], [wbb] + hT_T(c, T))
                    DVE.do(lambda e, ta=ta, cbbk=cbbk, f=f, cols_T=cols_T: e.tensor_tensor(
                        out=y_v[:, f, cols_T], in0=cbbk[:], in1=ta[:], op=ALU.mult),
                        outs=[cd["y"][(f, T)]], ins=[cbbb, tab])
                    yield
                wfree(wc_)

        scale_xa = 1.0 / 16.0
        xa_w = {}

        def gen_xa_units(c, hx):
            hv = c.H.fm()
            rP, rQ = Ring(bk[0:6]), Ring([(bank_tr32, tr_b)])
            if hx % 2 == 0:
                for w_ in xa_w.values():
                    wfree(w_)
                xa_w["q"] = wload(w_in[:, OFF_XQ + (hx // 2) * 512:OFF_XQ + (hx // 2 + 1) * 512])
            wxq = xa_w["q"][0:2]
            set_ = hx % 2
            xq = CD[:, set_ * 4096:(set_ + 1) * 4096].rearrange("p (a b) -> p a b", a=2)
            def proj(T, c_):
                cols_T = slice(T * 512, (T + 1) * 512)
                fl = (2 * hx + c_) % 4
                pbk, pbb = rP.get()
                mm(pbk[:], [(wxq[0][:, kc, fl * 128:(fl + 1) * 128], hv[:, kc, cols_T]) for kc in range(KC)],
                   [pbb], [wxq[1]] + hT_T(c, T))
                sq, sqb = sqring.get()
                act(sq[:], pbk[:], AF.Square, [sqb], [pbb])
                return pbk, pbb, sq, sqb

            def finish(T, pcs):
                xqb = [cd["kq"][(2 * set_, T)], cd["kq"][(2 * set_ + 1, T)]]
                cols_T = slice(T * 512, (T + 1) * 512)
                qb_, qbb = rQ.get()
                mm(qb_[:], [(ones_bf[:], pcs[c_][2][:]) for c_ in range(2)], [qbb], [pcs[0][3], pcs[1][3], const_b])
                t1, t1b = tring.get()
                act(t1[:], qb_[:], AF.Ln, [t1b], [qbb], scale=1.0 / 256, bias=EPS)
                act(t1[:], t1[:], AF.Exp, [t1b], [t1b], scale=-0.5)
                for c_ in range(2):
                    pbk, pbb = pcs[c_][0:2]
                    DVE.do(lambda e, pbk=pbk, c_=c_, t1=t1, cols_T=cols_T: e.scalar_tensor_tensor(
                        out=xq[:, c_, cols_T], in0=pbk[:], scalar=col(34 + c_), in1=t1[:],
                        op0=ALU.mult, op1=ALU.mult), outs=[xqb[c_]], ins=[pbb, t1b, cols_b])

            pend = None
            for T in range(4):
                p0 = proj(T, 0)
                if pend is not None:
                    finish(T - 1, pend)
                p1 = proj(T, 1)
                pend = [p0, p1]
                yield
            finish(3, pend)
            yield
            if hx == 3:
                for w_ in xa_w.values():
                    wfree(w_)
                xa_w.clear()

        def gen_xa_attn(c, hx):
            set_ = hx % 2
            xq = CD[:, set_ * 4096:(set_ + 1) * 4096].rearrange("p (a b) -> p a b", a=2)
            rS, rO, rL = xa_rings

            def smm(T):
                xqb = [cd["kq"][(2 * set_, T)], cd["kq"][(2 * set_ + 1, T)]]
                cols_T = slice(T * 512, (T + 1) * 512)
                ps_ = []
                for mc in range(2):
                    sbk, sbb = rS.get()
                    mm(sbk[:], [(xa_kT[:, 2 * hx + c_, mc * 128:(mc + 1) * 128], xq[:, c_, cols_T]) for c_ in range(2)],
                       [sbb], [xak_b, xqb[0], xqb[1]])
                    p, pbuf = pring.get()
                    act(p[:], sbk[:], AF.Exp, [pbuf], [sbb], scale=scale_xa)
                    ps_.append((p, pbuf))
                return ps_

            cur = smm(0)
            for T in range(4):
                cols_T = slice(T * 512, (T + 1) * 512)
                ps_ = cur
                cur = smm(T + 1) if T + 1 < 4 else None
                lbk, lbb = rL.get()
                mm(lbk[:], [(ones_bf[:], ps_[mc][0][:]) for mc in range(2)], [lbb], [ps_[0][1], ps_[1][1], const_b])
                t1, t1b = tring.get()
                act(t1[:], lbk[:], AF.Ln, [t1b], [lbb])
                act(t1[:], t1[:], AF.Exp, [t1b], [t1b], scale=-1.0)
                evac = []
                for c_ in range(2):
                    f = 2 * hx + c_
                    obk, obb = rO.get()
                    mm(obk[:], [(xa_V[:, mc, f * 128:(f + 1) * 128], ps_[mc][0][:]) for mc in range(2)],
                       [obb], [xav_b, ps_[0][1], ps_[1][1]])
                    t2, t2b = tring.get()
                    DVE.do(lambda e, t2=t2, obk=obk: e.tensor_copy(out=t2[:], in_=obk[:]), outs=[t2b], ins=[obb])
                    evac.append((f, t2, t2b))
                for f, t2, t2b in evac:
                    DVE.do(lambda e, t1=t1, t2=t2, f=f, cols_T=cols_T: e.tensor_tensor(
                        out=y_v[:, f, cols_T], in0=t2[:], in1=t1[:], op=ALU.mult),
                        outs=[cd["y"][(f, T)]], ins=[t2b, t1b])
                yield

        wo_w = {}

        def gen_wo(c, tiles):
            b = c.b
            Mv = c.G.fm()
            hv = c.H.fm()
            load_gain(n2g_d)
            wo_c = [wload(wo_d[:, cc_ * 512:(cc_ + 1) * 512]) for cc_ in range(2)]

            def wo_load(t):
                xt, xtb, slot = xring.get()
                r0 = c.xrow0 + t * 128
                SP.dma(lambda e, xt=xt, r0=r0: e.dma_start(out=xt[:], in_=x_d[r0:r0 + 128, :]), slot, outs=[xtb])
                return xt, xtb, slot

            tiles = list(tiles)
            n = len(tiles)
            mm_res = {}
            xs = {}
            sts = {}
            xns = {}

            def do_mm(t):
                res = []
                for half in range(2):
                    pbk, pbb = ringP.get()
                    mm(pbk[:], [(Mv[:, kc, t * 128:(t + 1) * 128], wo_c[half][0][:, kc, :]) for kc in range(KC)],
                       [pbb], [wo_c[half][1]] + [c.Mt[(kc, t // 4)] for kc in range(KC)])
                    res.append((pbk, pbb))
                mm_res[t] = res

            def stage1(t):
                xt, xtb, slot = xs[t]
                r0 = c.xrow0 + t * 128
                for half in range(2):
                    pbk, pbb = mm_res[t][half]
                    DVE.do(lambda e, xt=xt, pbk=pbk, half=half: e.tensor_tensor(
                        out=xt[:, half * 512:(half + 1) * 512], in0=pbk[:], in1=xt[:, half * 512:(half + 1) * 512],
                        op=ALU.add), outs=[xtb], ins=[pbb])
                x1b = Buf()
                x1d_t[(b, t)] = x1b
                SP.dma(lambda e, xt=xt, r0=r0: e.dma_start(out=out_d[r0:r0 + 128, :], in_=xt[:]), slot,
                       outs=[x1b], ins=[xtb])
                sts[t] = norm_A1(xt, xtb)

            xs[tiles[0]] = wo_load(tiles[0])
            if n > 1:
                xs[tiles[1]] = wo_load(tiles[1])
            do_mm(tiles[0])
            if n > 1:
                do_mm(tiles[1])
            stage1(tiles[0])
            for i, t in enumerate(tiles):
                if i + 2 < n:
                    xs[tiles[i + 2]] = wo_load(tiles[i + 2])
                    do_mm(tiles[i + 2])
                if i + 1 < n:
                    stage1(tiles[i + 1])
                xt, xtb, slot = xs[t]
                st, stb = sts[t]
                xns[t] = norm_A2(xt, xtb, st, stb)
                if i >= 1:
                    tp = tiles[i - 1]
                    norm_B(xns[tp][0], xns[tp][1], hv, [c.hT[tp]], tp * 128)
                yield
            tp = tiles[-1]
            norm_B(xns[tp][0], xns[tp][1], hv, [c.hT[tp]], tp * 128)
            for w_ in wo_c:
                wfree(w_)

        def ffn_start(c):
            ainit = retire(list(cd["y"].values())) + retire(list(cd["kq"].values())) + retire(list(cd["aT"].values())) + retire([cd["memT"]])
            cd["aT"] = {(f, T2): Buf(init=ainit) for f in range(NF) for T2 in range(2)}

        def ld_out(half_, rg):
            kcn = 8 if rg < 2 else 6
            return wload(wfo[rg * 1024:rg * 1024 + kcn * 128, half_ * 512:(half_ + 1) * 512], kcn=kcn)

        def gen_ffn_in(c, hb):
            hv = c.H.fm()
            tok0 = hb * 1024
            ffn_start(c)
            aT_t = cd["aT"]
            order = (5, 0, 1, 2, 3, 4)

            def ld(ci):
                ncols_ = (4 if ci < 5 else 2) * 128
                return (wload(wfi[:, ci * 512:ci * 512 + ncols_], ncols=ncols_),
                        wload(wfi[:, DFF + ci * 512:DFF + ci * 512 + ncols_], ncols=ncols_))

            nxt_p = None
            for oi, ci in enumerate(order):
                nfl = 4 if ci < 5 else 2
                ncols = nfl * 128
                wg, wu = nxt_p if nxt_p is not None else ld(ci)
                nxt_p = None
                if oi + 1 < len(order) and wnfree() >= 2:
                    nxt_p = ld(order[oi + 1])
                elif oi + 1 == len(order):
                    c.pre_out = {}
                    for rg in range(3):
                        if wnfree() >= 1:
                            c.pre_out[(0, rg)] = ld_out(0, rg)
                for fl in range(nfl):
                    f = 4 * ci + fl
                    for T2 in range(2):
                        tc = slice(tok0 + T2 * 512, tok0 + (T2 + 1) * 512)
                        t0_ = (tok0 // 128) + 4 * T2
                        hts = [c.hT[t] for t in range(t0_, t0_ + 4)]
                        gbk, gbb = ringP.get()
                        mm(gbk[:], [(wg[0][:, kc, fl * 128:(fl + 1) * 128], hv[:, kc, tc]) for kc in range(KC)],
                           [gbb], [wg[1]] + hts)
                        sg, sgb = tring.get()
                        act(sg[:], gbk[:], AF.Silu, [sgb], [gbb])
                        ubk, ubb = ringP.get()
                        mm(ubk[:], [(wu[0][:, kc, fl * 128:(fl + 1) * 128], hv[:, kc, tc]) for kc in range(KC)],
                           [ubb], [wu[1]] + hts)
                        DVE.do(lambda e, ubk=ubk, sg=sg, f=f, T2=T2: e.tensor_tensor(
                            out=aT_v[:, f, T2 * 512:(T2 + 1) * 512], in0=ubk[:], in1=sg[:], op=ALU.mult),
                            outs=[aT_t[(f, T2)]], ins=[ubb, sgb])
                        yield
                wfree(wg)
                wfree(wu)

        def gen_ffn_out(c, hb):
            b = c.b
            aT_t = cd["aT"]
            pre_out = getattr(c, "pre_out", None) or {}
            c.pre_out = None
            for half in range(2):
                wch = [pre_out.pop((half, rg)) if (half, rg) in pre_out else ld_out(half, rg) for rg in range(3)]
                if half == 0:
                    for rg in range(2):
                        if wnfree() >= 1:
                            pre_out[(1, rg)] = ld_out(1, rg)
                    s_x = wfo[2048:2048 + 6 * 128, 512:1024].rearrange("(kc p) n -> p kc n", p=128)
                    POOL.dma(lambda e, s_x=s_x: e.dma_start(out=wx_v[:, 0:6, :], in_=s_x), wx_slot,
                             outs=[wx_b], extra=xak_b.all_toks() + xav_b.all_toks())
                    pre_out[(1, 2)] = (wx_v, wx_b, -1)
                elif hb == 1 and c.b + 1 < NB and wnfree() >= 2:
                    ctxs[c.b + 1].pre_fv = [wload(w_in[:, OFF_FV + cc_ * 512:OFF_FV + (cc_ + 1) * 512]) for cc_ in range(2)]

                def f_load(tl, half=half):
                    t = hb * 8 + tl
                    r0 = c.xrow0 + t * 128
                    xt, xtb, slot = xring.get()
                    SP.dma(lambda e, xt=xt, r0=r0, half=half: e.dma_start(
                        out=xt[:, 0:512], in_=out_d[r0:r0 + 128, half * 512:(half + 1) * 512]), slot,
                        outs=[xtb], ins=[x1d_t[(b, t)]])
                    return xt, xtb, slot

                nxt_x = f_load(0)
                for tl in range(8):
                    t = hb * 8 + tl
                    r0 = c.xrow0 + t * 128
                    xt, xtb, slot = nxt_x
                    if tl + 1 < 8:
                        nxt_x = f_load(tl + 1)
                    pbk, pbb = ringP.get()
                    for rg in range(3):
                        fs = range(8 * rg, min(8 * rg + 8, NF))
                        mm(pbk[:], [(aT_v[:, f, tl * 128:(tl + 1) * 128], wch[rg][0][:, f % 8, :]) for f in fs],
                           [pbb], [wch[rg][1]] + [aT_t[(f, tl // 4)] for f in fs], start=(rg == 0), stop=(rg == 2))
                    DVE.do(lambda e, xt=xt, pbk=pbk: e.tensor_tensor(
                        out=xt[:, 0:512], in0=pbk[:], in1=xt[:, 0:512], op=ALU.add), outs=[xtb], ins=[pbb])
                    ot = SP.dma(lambda e, xt=xt, r0=r0, half=half: e.dma_start(
                        out=out_d[r0:r0 + 128, half * 512:(half + 1) * 512], in_=xt[:, 0:512]), slot,
                        outs=[x1d_t[(b, t)]], ins=[xtb])
                    out_toks.append(ot)
                    yield
                for w_ in wch:
                    if w_[2] >= 0:
                        wfree(w_)

        def xa_start(c):
            kqi = retire(list(cd["kq"].values())) + retire([cd["memT"]])
            cd["kq"] = {(s_, T): Buf(init=kqi) for s_ in range(4) for T in range(4)}

        ctxs = [make_ctx(b) for b in range(NB)]
        for b in range(NB):
            c = ctxs[b]
            if b == 0:
                n1 = gen_N1(c)
                next(n1)
                next(n1)
                next(n1)
                next(n1)
                fv0 = gen_FV(c)
                next(fv0)
                late_consts()
                interleave(n1, fv0, every=1)
                late_consts_finish()
            else:
                run(gen_FV(c))
            run(gen_FV_final(c))
            fox_start(c)
            for h in range(8):
                run(gen_fox_units(c, h))
                run(gen_fox_attn(c, h))
            interleave(gen_merge(c, "fox", OFF_GB, True), gen_M(c), every=3)
            run(gen_conv(c))
            run(gen_merge(c, "conv", OFF_GA, False))
            xa_start(c)
            for hx in range(4):
                run(gen_xa_units(c, hx))
                run(gen_xa_attn(c, hx))
            run(gen_merge(c, "xa", OFF_GC, False))
            run(gen_wo(c, range(NT)))
            nxt_n1 = gen_N1(ctxs[b + 1]) if b + 1 < NB else None
            interleave(chain(gen_ffn_in(c, 0), gen_ffn_out(c, 0), gen_ffn_in(c, 1), gen_ffn_out(c, 1)), nxt_n1, every=6, lag=3)


        SP.wait(out_toks)
        for it in wring.items:
            POOL.wait(it[1].all_toks())
        POOL.wait(wf_b.all_toks())

        with nc.Block() as block:
            @block.sync
            def _(e):
                SP.replay(e)

            @block.scalar
            def _(e):
                ACT.replay(e)

            @block.vector
            def _(e):
                DVE.replay(e)

            @block.tensor
            def _(e):
                PE.replay(e)

            @block.gpsimd
            def _(e):
                POOL.replay(e)
    return nc


_NC_CACHE = {}


def _consts():
    ident = np.eye(128, dtype=np.float32)
    s = np.arange(128)
    utri = (s[:, None] <= s[None, :]).astype(np.float32)
    j = s // 8
    h = s % 8
    same_h = h[:, None] == h[None, :]
    mjj = (same_h & (j[:, None] < j[None, :])).astype(np.float32)
    mjji = (same_h & (j[:, None] <= j[None, :])).astype(np.float32)
    mneg = np.where(s[:, None] > s[None, :], NEG, 0.0).astype(np.float32)
    return {"c_ident": ident, "c_utri": utri, "c_mjj": mjj, "c_mjji": mjji, "c_mneg": mneg}


def kernel(**inputs):
    if "nc" not in _NC_CACHE:
        _NC_CACHE["nc"] = build_program()
    nc = _NC_CACHE["nc"]
    f32 = lambda a: np.ascontiguousarray(np.asarray(a, dtype=np.float32))
    x = f32(inputs["x"])
    mem = f32(inputs["mem"])
    shared = {
        "norm1_g": f32(inputs["norm1_g"]).reshape(1, D),
        "w_in": f32(inputs["w_in"]).reshape(D, IN_COLS),
        "conv_w": f32(inputs["conv_w"]).reshape(3, D),
        "conv_b": f32(inputs["conv_b"]).reshape(1, D),
        "fox_f_bias": f32(inputs["fox_f_bias"]).reshape(1, 8),
        "fox_q_g": f32(inputs["fox_q_g"]).reshape(1, 128),
        "fox_k_g": f32(inputs["fox_k_g"]).reshape(1, 128),
        "mem_norm_g": f32(inputs["mem_norm_g"]).reshape(1, D),
        "w_mem_kv": f32(inputs["w_mem_kv"]).reshape(D, 2 * D),
        "xa_q_g": f32(inputs["xa_q_g"]).reshape(1, 256),
        "xa_k_g": f32(inputs["xa_k_g"]).reshape(1, 256),
        "w_br_conv": f32(inputs["w_br_conv"]).reshape(D, D),
        "w_br_fox": f32(inputs["w_br_fox"]).reshape(D, D),
        "w_br_xa": f32(inputs["w_br_xa"]).reshape(D, D),
        "w_o": f32(inputs["w_o"]).reshape(D, D),
        "norm2_g": f32(inputs["norm2_g"]).reshape(1, D),
        "w_ffn_in": f32(inputs["w_ffn_in"]).reshape(D, 2 * DFF),
        "w_ffn_out": f32(inputs["w_ffn_out"]).reshape(DFF, D),
    }
    shared.update(_consts())
    in_maps = []
    for c in range(N_CORES):
        m = dict(shared)
        m["x"] = np.ascontiguousarray(x[NB * c:NB * (c + 1)].reshape(NB * S, D))
        m["mem"] = np.ascontiguousarray(mem[NB * c:NB * (c + 1)].reshape(NB * NMEM, D))
        in_maps.append(m)
    res = run_bass_kernel_spmd(nc, in_maps, core_ids=list(range(N_CORES)))
    out = np.concatenate([np.asarray(r["out"]).reshape(NB, S, D) for r in res.results], axis=0)
    return out.astype(np.float32)
```

```python
import math
from contextlib import ExitStack

import numpy as np
import concourse.bass as bass
import concourse.mybir as mybir
from concourse.bass_utils import run_bass_kernel_spmd

F32 = mybir.dt.float32
BF16 = mybir.dt.bfloat16
AF = mybir.ActivationFunctionType
ALU = mybir.AluOpType

N_CORES = 8
D = 1024
S = 2048
NB = 2
NT = S // 128
KC = D // 128
NMEM = 256
DFF = 2816
NF = DFF // 128
IN_COLS = 10248
OFF_CB, OFF_CC, OFF_CV, OFF_FQ, OFF_FK, OFF_FV, OFF_XQ, OFF_GA, OFF_GB, OFF_GC, OFF_FF = (
    0, 1024, 2048, 3072, 4096, 5120, 6144, 7168, 8192, 9216, 10240)
EPS = 1e-6
NEG = -30000.0


class Tok:
    __slots__ = ("sem", "val")

    def __init__(self, sem, val):
        self.sem = sem
        self.val = val


class Buf:
    __slots__ = ("w", "r")

    def __init__(self, init=()):
        self.w = None
        self.r = {}
        for t in init:
            self.add_r(t)

    def add_r(self, t):
        if t is None:
            return
        k = id(t.sem)
        o = self.r.get(k)
        if o is None or o.val < t.val:
            self.r[k] = t

    def all_toks(self):
        return ([self.w] if self.w is not None else []) + list(self.r.values())


def retire(bufs):
    out = Buf()
    for b in bufs:
        for t in b.all_toks():
            out.add_r(t)
    return list(out.r.values())


class Eng:
    def __init__(self, name, sem, skip_self=False):
        self.name = name
        self.sem = sem
        self.cnt = 0
        self.ops = []
        self.waited = {}
        self.skip_self = skip_self

    def wait(self, toks):
        for t in toks:
            if t is None:
                continue
            if self.skip_self and t.sem is self.sem:
                continue
            k = id(t.sem)
            if self.waited.get(k, 0) >= t.val:
                continue
            self.waited[k] = t.val
            self.ops.append(lambda e, t=t: e.wait_ge(t.sem, t.val))

    @staticmethod
    def _deps(outs, ins, extra):
        deps = list(extra)
        for b in ins:
            deps.append(b.w)
        for b in outs:
            deps.extend(b.all_toks())
        return deps

    @staticmethod
    def _reg(tok, outs, ins):
        for b in ins:
            b.add_r(tok)
        for b in outs:
            b.w = tok
            b.r = {}

    def do(self, fn, outs=(), ins=(), extra=()):
        self.wait(self._deps(outs, ins, extra))
        self.cnt += 1
        tok = Tok(self.sem, self.cnt)
        sem = self.sem
        self.ops.append(lambda e: fn(e).then_inc(sem, 1))
        self._reg(tok, outs, ins)
        return tok

    def group(self, fns, outs=(), ins=(), extra=()):
        self.wait(self._deps(outs, ins, extra))
        for fn in fns[:-1]:
            self.ops.append(lambda e, fn=fn: fn(e))
        self.cnt += 1
        tok = Tok(self.sem, self.cnt)
        sem = self.sem
        last = fns[-1]
        self.ops.append(lambda e: last(e).then_inc(sem, 1))
        self._reg(tok, outs, ins)
        return tok

    def dma(self, fn, slot, outs=(), ins=(), extra=()):
        self.wait(self._deps(outs, ins, extra))
        slot[1] += 16
        tok = Tok(slot[0], slot[1])
        s = slot[0]
        self.ops.append(lambda e: fn(e).then_inc(s, 16))
        self._reg(tok, outs, ins)
        return tok

    def replay(self, e):
        for o in self.ops:
            o(e)


class Ring:
    def __init__(self, items):
        self.items = items
        self.i = 0

    def get(self):
        it = self.items[self.i % len(self.items)]
        self.i += 1
        return it


def build_program():
    nc = bass.Bass("TRN2", target_bir_lowering=False)

    def din(name, shape):
        return nc.dram_tensor(name, shape, F32, kind="ExternalInput").ap()

    x_d = din("x", [NB * S, D])
    mem_d = din("mem", [NB * NMEM, D])
    n1g_d = din("norm1_g", [1, D])
    w_in = din("w_in", [D, IN_COLS])
    conv_w_d = din("conv_w", [3, D])
    conv_b_d = din("conv_b", [1, D])
    fbias_d = din("fox_f_bias", [1, 8])
    fqg_d = din("fox_q_g", [1, 128])
    fkg_d = din("fox_k_g", [1, 128])
    mng_d = din("mem_norm_g", [1, D])
    wmkv = din("w_mem_kv", [D, 2 * D])
    xqg_d = din("xa_q_g", [1, 256])
    xkg_d = din("xa_k_g", [1, 256])
    wbr = {"conv": din("w_br_conv", [D, D]), "fox": din("w_br_fox", [D, D]), "xa": din("w_br_xa", [D, D])}
    wo_d = din("w_o", [D, D])
    n2g_d = din("norm2_g", [1, D])
    wfi = din("w_ffn_in", [D, 2 * DFF])
    wfo = din("w_ffn_out", [DFF, D])
    c_ident = din("c_ident", [128, 128])
    c_utri = din("c_utri", [128, 128])
    c_mjj = din("c_mjj", [128, 128])
    c_mjji = din("c_mjji", [128, 128])
    c_mneg = din("c_mneg", [128, 128])
    out_d = nc.dram_tensor("out", [NB * S, D], F32, kind="ExternalOutput").ap()

    with ExitStack() as es:
        def sb(name, shape, dt):
            return es.enter_context(nc.sbuf_tensor(name, shape, dt))

        def ps(name, shape, dt):
            return es.enter_context(nc.psum_tensor(name, shape, dt))

        def sem(name):
            return es.enter_context(nc.semaphore(name))

        PE = Eng("pe", sem("s_pe"), skip_self=True)
        ACT = Eng("act", sem("s_act"))
        DVE = Eng("dve", sem("s_dve"))
        POOL = Eng("pool", sem("s_pool"))
        SP = Eng("sp", sem("s_sp"))

        A = sb("A", [128, 16384], BF16)
        Bt = sb("B", [128, 16384], BF16)
        CD = sb("CD", [128, 24576], BF16)
        y_v = CD[:, 8192:24576].rearrange("p (a b) -> p a b", a=KC)
        aT_v = CD[:, 0:NF * 1024].rearrange("p (a b) -> p a b", a=NF)
        memT_v = CD[:, 0:2048].rearrange("p (a b) -> p a b", a=KC)

        NW = 5
        wbufs = [sb(f"w{i}", [128, KC, 512], BF16) for i in range(NW)]
        wring = Ring([(wbufs[i], Buf(), [sem(f"dw{i}"), 0]) for i in range(NW)])

        xts = [sb(f"xt{i}", [128, D], F32) for i in range(3)]
        xring = Ring([(xts[i], Buf(), [sem(f"dx{i}"), 0]) for i in range(3)])
        gt = sb("gt", [128, D], F32)
        gt_b = Buf()
        gt_slot = [sem("dgt"), 0]
        xnbs = [sb(f"xnb{i}", [128, D], BF16) for i in range(2)]
        xnring = Ring([(xnbs[i], Buf()) for i in range(2)])
        NTMP = 5
        tmps = [sb(f"tmp{i}", [128, 512], F32) for i in range(NTMP)]
        tring = Ring([(tmps[i], Buf()) for i in range(NTMP)])
        pbs = [sb(f"pb{i}", [128, 512], BF16) for i in range(4)]
        pring = Ring([(pbs[i], Buf()) for i in range(4)])
        sqs = [sb(f"sq{i}", [128, 512], BF16) for i in range(3)]
        sqring = Ring([(sqs[i], Buf()) for i in range(3)])
        us = [sb(f"u{i}", [128, 514], F32) for i in range(2)]
        uring = Ring([(us[i], Buf()) for i in range(2)])
        st_small = [sb(f"st{i}", [128, 4], F32) for i in range(4)]
        string = Ring([(st_small[i], Buf()) for i in range(4)])

        ident32 = sb("ident32", [128, 128], F32)
        utri32 = sb("utri32", [128, 128], F32)
        mjj32 = sb("mjj32", [128, 128], F32)
        mjji32 = sb("mjji32", [128, 128], F32)
        ones32 = sb("ones32", [128, 128], F32)
        ident_bf = sb("ident_bf", [128, 128], BF16)
        mneg_bf = sb("mneg_bf", [128, 128], BF16)
        ones_bf = sb("ones_bf", [128, 128], BF16)
        rows = sb("rows", [38, 128], F32)
        cols = sb("cols", [128, 38], F32)
        fb_b = sb("fb_b", [128, NT, 8], F32)
        xakv = sb("xakv", [128, 4096], BF16)
        xa_kT = xakv[:, 0:2048].rearrange("p (a b) -> p a b", a=KC)
        xa_V = xakv[:, 2048:4096].rearrange("p (a b) -> p a b", a=2)
        wx_v = xakv[:].rearrange("p (a b) -> p a b", a=KC)
        wx_b = Buf()
        wx_slot = [sem("dwx"), 0]
        bias_all = sb("bias_all", [128, 40, 8], F32)
        wf_sb = sb("wf_sb", [128, KC, 8], BF16)
        const_b = Buf()
        cols_b = Buf()
        xak_b = Buf()
        xav_b = Buf()
        bias_b = Buf()
        wf_b = Buf()
        wf_slot = [sem("dwf"), 0]

        banks = [ps(f"bank{i}", [128, 512], F32) for i in range(7)]
        bank_tr = ps("bank_tr", [128, KC, 128], BF16)
        bk = [(banks[i], Buf()) for i in range(7)]
        tr_b = Buf()
        ringP = Ring(bk[0:4])
        ringQ = Ring(bk[4:7])
        ringS = Ring(bk[0:3])
        ringO = Ring(bk[3:5])
        ringL = Ring(bk[5:7])

        hT_t = [Buf() for _ in range(NT)]
        B_V_t = [Buf() for _ in range(NT)]
        B_M_t = {}
        kq_t = [Buf() for _ in range(4)]
        y_t = {(f, T): Buf() for f in range(KC) for T in range(4)}
        memT_b = Buf()
        aT_t = {}
        x1d_t = {}
        out_toks = []

        wheld = [False] * NW
        wpos = [0]

        wfreed_at = [0] * NW
        wclock = [0]

        def wnfree():
            return sum(1 for h_ in wheld if not h_)

        def wget():
            cands = [i for i in range(NW) if not wheld[i]]
            if not cands:
                raise RuntimeError("no free weight buffer")
            i = min(cands, key=lambda j: wfreed_at[j])
            wheld[i] = True
            wb, b, slot = wring.items[i]
            return wb, b, slot, i

        def wfree(w):
            assert wheld[w[2]]
            wheld[w[2]] = False
            wclock[0] += 1
            wfreed_at[w[2]] = wclock[0]

        def wload(src, kcn=KC, ncols=512):
            wb, b, slot, i = wget()
            dst = wb[:, 0:kcn, 0:ncols]
            s = src.rearrange("(kc p) n -> p kc n", p=128)
            POOL.dma(lambda e: e.dma_start(out=dst, in_=s), slot, outs=[b])
            return wb, b, i

        def wload3(c0s, ncols=128):
            wb, b, slot, i = wget()
            first = True
            for j, c0 in enumerate(c0s):
                dst = wb[:, :, j * ncols:(j + 1) * ncols]
                s = w_in[:, c0:c0 + ncols].rearrange("(kc p) n -> p kc n", p=128)
                if first:
                    POOL.dma(lambda e, dst=dst, s=s: e.dma_start(out=dst, in_=s), slot, outs=[b])
                    first = False
                else:
                    slot[1] += 16
                    tok = Tok(slot[0], slot[1])
                    sl = slot[0]
                    POOL.ops.append(lambda e, dst=dst, s=s: e.dma_start(out=dst, in_=s).then_inc(sl, 16))
                    b.w = tok
            return wb, b, i

        def mm(out_ap, pairs, outs, ins, start=True, stop=True):
            n = len(pairs)
            fns = []
            for i, (l, r) in enumerate(pairs):
                fns.append(lambda e, l=l, r=r, i=i: e.matmul(out_ap, lhsT=l, rhs=r,
                                                             start=(start and i == 0), stop=(stop and i == n - 1)))
            return PE.group(fns, outs=outs, ins=ins)

        def act(out, in_, func, outs, ins, **kw):
            return ACT.do(lambda e: e.activation(out=out, in_=in_, func=func, **kw), outs=outs, ins=ins)

        def rstd_small(ssq_ap, ssq_b, n):
            st, stb = string.get()
            act(st[:, 0:1], ssq_ap, AF.Ln, [stb], [ssq_b], scale=1.0 / n, bias=EPS)
            act(st[:, 1:2], st[:, 0:1], AF.Exp, [stb], [stb], scale=-0.5)
            return st[:, 1:2], stb

        def load_gain(src):
            SP.dma(lambda e: e.dma_start(out=gt[:], in_=src.partition_broadcast(128)), gt_slot, outs=[gt_b])

        def norm_tile(xt, xtb, dstT, dst_bufs, ncol0):
            xn, xnb = norm_A(xt, xtb)
            norm_B(xn, xnb, dstT, dst_bufs, ncol0)

        junk_ap = us[0][:].bitcast(BF16)[:, 0:D]
        junk_b = uring.items[0][1]

        def norm_A1(xt, xtb):
            st, stb = string.get()
            act(junk_ap, xt[:], AF.Square, [junk_b, stb], [xtb], accum_out=st[:, 2:3])
            act(st[:, 0:1], st[:, 2:3], AF.Ln, [stb], [stb], scale=1.0 / D, bias=EPS)
            act(st[:, 1:2], st[:, 0:1], AF.Exp, [stb], [stb], scale=-0.5)
            return st, stb

        def norm_A2(xt, xtb, st, stb):
            xn, xnb = xnring.get()
            DVE.do(lambda e: e.scalar_tensor_tensor(out=xn[:], in0=xt[:], scalar=st[:, 1:2], in1=gt[:],
                                                    op0=ALU.mult, op1=ALU.mult),
                   outs=[xnb], ins=[xtb, stb, gt_b])
            return xn, xnb

        def norm_A(xt, xtb):
            st, stb = norm_A1(xt, xtb)
            return norm_A2(xt, xtb, st, stb)

        def norm_B(xn, xnb, dstT, dst_bufs, ncol0):
            fns = [lambda e, kc=kc: e.transpose(out=bank_tr[:, kc, :], in_=xn[:, kc * 128:(kc + 1) * 128],
                                                identity=ident_bf[:]) for kc in range(KC)]
            PE.group(fns, outs=[tr_b], ins=[xnb, const_b])
            act(dstT[:, :, ncol0:ncol0 + 128], bank_tr[:], AF.Copy, dst_bufs, [tr_b])

        cslotA = [sem("dconstA"), 0]
        cslotB = [sem("dconstB"), 0]
        const2_b = Buf()
        rows_b = const2_b
        SP.dma(lambda e: e.dma_start(out=ident32[:], in_=c_ident), cslotA, outs=[const_b])
        mstage, mstage_b = tring.get()
        cdmas = [(utri32[:], c_utri), (mjj32[:], c_mjj), (mjji32[:], c_mjji),
                 (mstage[:, 0:128], c_mneg),
                 (rows[0:24, :], conv_w_d.rearrange("k (f p) -> (k f) p", p=128)),
                 (rows[24:32, :], conv_b_d.rearrange("o (f p) -> (o f) p", p=128)),
                 (rows[32:33, :], fqg_d), (rows[33:34, :], fkg_d),
                 (rows[34:36, :], xqg_d.rearrange("o (c p) -> (o c) p", p=128)),
                 (rows[36:38, :], xkg_d.rearrange("o (c p) -> (o c) p", p=128)),
                 (fb_b[:], bass.AP(fbias_d.tensor, 0, [[0, 128], [0, NT], [1, 8]]))]
        deferred_const = []
        for dst, src in cdmas:
            cslotB[1] += 16
            slB = cslotB[0]
            deferred_const.append(lambda e, dst=dst, src=src: e.dma_start(out=dst, in_=src).then_inc(slB, 16))
        const2_b.w = Tok(cslotB[0], cslotB[1])
        mstage_b.w = const2_b.w
        DVE.do(lambda e: e.tensor_copy(out=ident_bf[:], in_=ident32[:]), outs=[const_b], ins=[const_b])
        DVE.do(lambda e: e.memset(ones32[:], 1.0), outs=[const_b], ins=[const_b])
        DVE.do(lambda e: e.memset(ones_bf[:], 1.0), outs=[const_b], ins=[const_b])
        DVE.do(lambda e: e.memset(bias_all[:], 0.0), outs=[bias_b], ins=[])

        def late_consts():
            for fn in deferred_const:
                POOL.ops.append(fn)

        def late_consts_finish():
            DVE.do(lambda e: e.tensor_copy(out=mneg_bf[:], in_=mstage[:, 0:128]), outs=[const2_b], ins=[const2_b, mstage_b])
            pb0, pb0b = ringQ.get()
            mm(pb0[:, 0:38], [(rows[0:38, :], ident32[0:38, 0:38])], [pb0b], [rows_b, const_b])
            DVE.do(lambda e: e.tensor_copy(out=cols[:], in_=pb0[:, 0:38]), outs=[cols_b], ins=[pb0b])

        def col(i):
            return cols[:, i:i + 1]

        class Region:
            def __init__(self, t):
                self.t = t
                self.trk = {}

            def view(self, keys):
                init = retire(list(self.trk.values()))
                self.trk = {k: Buf(init=init) for k in keys}
                return self.trk

            def fm(self):
                return self.t[:].rearrange("p (a b) -> p a b", a=KC)

            def tm(self):
                return self.t[:].rearrange("p (a b) -> p a b", a=NT)

        regs = [Region(A), Region(Bt)]
        cd = {"kq": {(s_, T): Buf() for s_ in range(4) for T in range(4)}, "memT": Buf(),
              "y": {(f, T): Buf() for f in range(KC) for T in range(4)}, "aT": {}}

        class Ctx:
            pass

        def make_ctx(b):
            c = Ctx()
            c.b = b
            c.xrow0 = b * S
            c.H = regs[b % 2]
            c.G = regs[(b + 1) % 2]
            c.hT = None
            c.Vt = None
            c.Mt = None
            c.bidx = {}
            return c

        def run(gen):
            if gen is None:
                return
            for _ in gen:
                pass

        def interleave(main, side, every, lag=0):
            n = 0
            side_live = side is not None
            for _ in main:
                n += 1
                if side_live and n > lag and (n - lag) % every == 0:
                    try:
                        next(side)
                    except StopIteration:
                        side_live = False
            if side_live:
                for _ in side:
                    pass

        def chain(*gens):
            for g in gens:
                if g is None:
                    continue
                for x in g:
                    yield x

        def gen_M(c):
            b = c.b
            load_gain(mng_d)
            cd["memT"] = Buf(init=retire(list(cd["kq"].values())) + retire(list(cd["aT"].values())))
            memT_b = cd["memT"]
            for t_ in wx_b.all_toks():
                xak_b.add_r(t_)
                xav_b.add_r(t_)
            wcur = wload(wmkv[:, 0:512])
            prevB = None
            for mt in range(2):
                xt, xtb, slot = xring.get()
                r0 = b * NMEM + mt * 128
                SP.dma(lambda e, xt=xt, r0=r0: e.dma_start(out=xt[:], in_=mem_d[r0:r0 + 128, :]), slot, outs=[xtb])
                xn, xnb = norm_A(xt, xtb)
                if prevB is not None:
                    norm_B(prevB[0], prevB[1], memT_v, [memT_b], prevB[2] * 128)
                prevB = (xn, xnb, mt)
                yield
            norm_B(prevB[0], prevB[1], memT_v, [memT_b], prevB[2] * 128)
            yield
            rP, rQ = Ring(bk[4:6]), Ring(bk[6:7])
            for hx in range(4):
                pcs = []
                sqc = []
                for c_ in range(2):
                    f = 2 * hx + c_
                    wb, wbb = wcur[0:2]
                    pbk, pbb = rP.get()
                    mm(pbk[:, 0:NMEM], [(wb[:, kc, (f % 4) * 128:(f % 4 + 1) * 128], memT_v[:, kc, :]) for kc in range(KC)],
                       [pbb], [wbb, memT_b])
                    sq, sqb = sqring.get()
                    act(sq[:, 0:NMEM], pbk[:, 0:NMEM], AF.Square, [sqb], [pbb])
                    pcs.append((pbk, pbb))
                    sqc.append((sq, sqb))
                qb_, qbb = rQ.get()
                mm(qb_[:, 0:NMEM], [(ones_bf[:], sqc[c_][0][:, 0:NMEM]) for c_ in range(2)], [qbb],
                   [sqc[0][1], sqc[1][1], const_b])
                t1, t1b = tring.get()
                act(t1[:, 0:NMEM], qb_[:, 0:NMEM], AF.Ln, [t1b], [qbb], scale=1.0 / 256, bias=EPS)
                act(t1[:, 0:NMEM], t1[:, 0:NMEM], AF.Exp, [t1b], [t1b], scale=-0.5)
                for c_ in range(2):
                    f = 2 * hx + c_
                    pbk, pbb = pcs[c_]
                    DVE.do(lambda e, pbk=pbk, f=f, c_=c_, t1=t1: e.scalar_tensor_tensor(
                        out=xa_kT[:, f, :], in0=pbk[:, 0:NMEM], scalar=col(36 + c_), in1=t1[:, 0:NMEM],
                        op0=ALU.mult, op1=ALU.mult), outs=[xak_b], ins=[pbb, t1b, cols_b])
                if hx == 1:
                    wfree(wcur)
                    wcur = wload(wmkv[:, 512:1024])
                if hx == 3:
                    wfree(wcur)
                    wcur = wload(wmkv[:, D:D + 512])
                yield
            for half in range(2):
                for mt in range(2):
                    wb, wbb = wcur[0:2]
                    pbk, pbb = rP.get()
                    mm(pbk[:], [(memT_v[:, kc, mt * 128:(mt + 1) * 128], wb[:, kc, :]) for kc in range(KC)],
                       [pbb], [wbb, memT_b])
                    act(xa_V[:, mt, half * 512:(half + 1) * 512], pbk[:], AF.Copy, [xav_b], [pbb])
                    if half == 0 and mt == 1:
                        wfree(wcur)
                        wcur = wload(wmkv[:, D + 512:D + 1024])
                    yield
            wfree(wcur)

        def gen_N1(c):
            load_gain(n1g_d)
            c.hT = c.H.view(range(NT))
            hv = c.H.fm()
            prevB = None
            for t in range(NT):
                xt, xtb, slot = xring.get()
                r0 = c.xrow0 + t * 128
                SP.dma(lambda e, xt=xt, r0=r0: e.dma_start(out=xt[:], in_=x_d[r0:r0 + 128, :]), slot, outs=[xtb])
                xn, xnb = norm_A(xt, xtb)
                if prevB is not None:
                    norm_B(prevB[0], prevB[1], hv, [c.hT[prevB[2]]], prevB[2] * 128)
                prevB = (xn, xnb, t)
                yield
            norm_B(prevB[0], prevB[1], hv, [c.hT[prevB[2]]], prevB[2] * 128)
            yield

        def hT_T(c, T):
            return [c.hT[t] for t in range(4 * T, 4 * T + 4)]

        def gen_FV(c):
            c.Vt = c.G.view(range(NT))
            hv = c.H.fm()
            Vv = c.G.tm()
            wv2 = [wload(w_in[:, OFF_FV + cc_ * 512:OFF_FV + (cc_ + 1) * 512]) for cc_ in range(2)]
            s_wf = w_in[:, OFF_FF:OFF_FF + 8].rearrange("(kc p) n -> p kc n", p=128)
            POOL.dma(lambda e: e.dma_start(out=wf_sb[:], in_=s_wf), wf_slot, outs=[wf_b])
            fbk, fbb = ringQ.get()
            c.fbk = (fbk, fbb)
            for t in range(NT):
                for half in range(2):
                    wb, wbb = wv2[half][0:2]
                    pbk, pbb = ringP.get()
                    mm(pbk[:], [(hv[:, kc, t * 128:(t + 1) * 128], wb[:, kc, :]) for kc in range(KC)],
                       [pbb], [wbb, c.hT[t]])
                    if half == 0:
                        act(Vv[:, t, half * 512:(half + 1) * 512], pbk[:], AF.Copy, [c.Vt[t]], [pbb])
                    else:
                        DVE.do(lambda e, t=t, half=half, pbk=pbk: e.tensor_copy(
                            out=Vv[:, t, half * 512:(half + 1) * 512], in_=pbk[:]), outs=[c.Vt[t]], ins=[pbb])
                mm(fbk[:, t * 8:(t + 1) * 8], [(hv[:, kc, t * 128:(t + 1) * 128], wf_sb[:, kc, :]) for kc in range(KC)],
                   [fbb], [wf_b, c.hT[t]])
                if t == NT - 1:
                    for w_ in wv2:
                        wfree(w_)
                    fox_w["k"] = wload(w_in[:, OFF_FK:OFF_FK + 512])
                    fox_w["q"] = wload(w_in[:, OFF_FQ:OFF_FQ + 512])
                yield

        def gen_FV_final(c):
            fbk, fbb = c.fbk
            fv, fvb = tring.get()
            lf_sb = fv[:, 0:128]
            z_sb = fv[:, 128:256]
            c_sb = fv[:, 256:384]
            cend_sb = fv[:, 384:512]
            DVE.do(lambda e: e.tensor_tensor(out=lf_sb, in0=fbk[:, 0:128], in1=fb_b[:].rearrange("p a b -> p (a b)"),
                                             op=ALU.add), outs=[fvb], ins=[fbb, const_b, const2_b])
            act(lf_sb, lf_sb, AF.Exp, [fvb], [fvb], scale=-1.0)
            act(lf_sb, lf_sb, AF.Ln, [fvb], [fvb], bias=1.0)
            zbk, zbb = ringQ.get()
            mm(zbk[:, 0:128], [(lf_sb, ones32[:])], [zbb], [fvb, const_b])
            act(z_sb, zbk[:, 0:128], AF.Copy, [fvb], [zbb])
            cbk, cbb = ringQ.get()
            mm(cbk[:, 0:128], [(utri32[:], lf_sb), (z_sb, mjj32[:])], [cbb], [fvb, const_b, const2_b])
            DVE.do(lambda e: e.tensor_copy(out=c_sb, in_=cbk[:, 0:128]), outs=[fvb], ins=[cbb])
            ebk, ebb = ringQ.get()
            mm(ebk[:, 0:128], [(z_sb, mjji32[:])], [ebb], [fvb, const_b, const2_b])
            DVE.do(lambda e: e.tensor_copy(out=cend_sb, in_=ebk[:, 0:128]), outs=[fvb], ins=[ebb])
            k = 0
            for qb in range(4):
                for j in range(4 * qb + 4):
                    c.bidx[(qb, j)] = k
                    jr = 4 * qb + 1
                    DVE.do(lambda e, k=k, j=j, jr=jr: e.tensor_tensor(
                        out=bias_all[:, k, :], in0=c_sb[:, j * 8:(j + 1) * 8], in1=cend_sb[:, jr * 8:(jr + 1) * 8],
                        op=ALU.subtract), outs=[bias_b], ins=[fvb])
                    k += 1
            yield

        scale_fox = 1.0 / math.sqrt(128.0)
        fox_w = {}
        bank_tr32 = bank_tr[:].rearrange("p a b -> p (a b)").bitcast(F32)
        fox_rings = (ringS, ringO, ringL)
        xa_rings = (Ring(bk[0:4]), Ring(bk[4:6]), Ring(bk[6:7]))

        def fox_start(c):
            kqi = retire([cd["memT"]]) + retire(list(cd["kq"].values()))
            cd["kq"] = {(s_, T): Buf(init=kqi) for s_ in range(4) for T in range(4)}
            y_init = retire(list(cd["aT"].values()))
            cd["y"] = {(f, T): Buf(init=retire([cd["y"][(f, T)]]) + y_init) for f in range(KC) for T in range(4)}

        def gen_fox_units(c, h):
            hv = c.H.fm()
            rP, rQ = Ring(bk[0:4]), Ring([(bank_tr32, tr_b), bk[6]])
            if h == 4:
                g = h // 4
                for w_ in fox_w.values():
                    wfree(w_)
                fox_w["k"] = wload(w_in[:, OFF_FK + g * 512:OFF_FK + (g + 1) * 512])
                fox_w["q"] = wload(w_in[:, OFF_FQ + g * 512:OFF_FQ + (g + 1) * 512])
            wk_c, wq_c = fox_w["k"][0:2], fox_w["q"][0:2]
            set_ = h % 2
            kslot, qslot = 2 * set_, 2 * set_ + 1
            kT = CD[:, kslot * 2048:(kslot + 1) * 2048]
            qT = CD[:, qslot * 2048:(qslot + 1) * 2048]
            units = [(which, T) for T in range(4) for which in (0, 1)]
            pend = None
            hl = h % 4
            for u in range(len(units) + 1):
                cur = None
                if u < len(units):
                    which, T = units[u]
                    wb, wbb = (wk_c, wq_c)[which]
                    pbk, pbb = rP.get()
                    mm(pbk[:], [(wb[:, kc, hl * 128:(hl + 1) * 128], hv[:, kc, T * 512:(T + 1) * 512]) for kc in range(KC)],
                       [pbb], [wbb] + hT_T(c, T))
                    sq, sqb = sqring.get()
                    act(sq[:], pbk[:], AF.Square, [sqb], [pbb])
                    cur = (which, T, pbk, pbb, sq, sqb)
                if pend is not None:
                    which, T, pbk, pbb, sq, sqb = pend
                    qb_, qbb = rQ.get()
                    mm(qb_[:], [(ones_bf[:], sq[:])], [qbb], [sqb, const_b])
                    t1, t1b = tring.get()
                    act(t1[:], qb_[:], AF.Ln, [t1b], [qbb], scale=1.0 / 128, bias=EPS)
                    act(t1[:], t1[:], AF.Exp, [t1b], [t1b], scale=-0.5)
                    dst = (kT, qT)[which][:, T * 512:(T + 1) * 512]
                    dstb = cd["kq"][((kslot, qslot)[which], T)]
                    gcol = col(33) if which == 0 else col(32)
                    DVE.do(lambda e, dst=dst, pbk=pbk, gcol=gcol, t1=t1: e.scalar_tensor_tensor(
                        out=dst, in0=pbk[:], scalar=gcol, in1=t1[:], op0=ALU.mult, op1=ALU.mult),
                        outs=[dstb], ins=[pbb, t1b, cols_b])
                pend = cur
                yield
            if h == 7:
                for w_ in fox_w.values():
                    wfree(w_)
                fox_w.clear()

        def gen_fox_attn(c, h):
            Vv = c.G.tm()
            set_ = h % 2
            kslot, qslot = 2 * set_, 2 * set_ + 1
            kT = CD[:, kslot * 2048:(kslot + 1) * 2048]
            qT = CD[:, qslot * 2048:(qslot + 1) * 2048]
            rS, rO, rL = fox_rings
            blocks = [(qb, j) for qb in range(4) for j in range(4 * qb + 4)]

            def qk(i):
                qb, j = blocks[i]
                r = j - 4 * qb
                c0 = max(r, 0) * 128
                sbk, sbb = rS.get()
                l_ap = kT[:, j * 128:(j + 1) * 128]
                r_ap = qT[:, qb * 512 + c0:(qb + 1) * 512]
                fns = [lambda e, sbk=sbk, c0=c0, l_ap=l_ap, r_ap=r_ap, r=r: e.matmul(
                    sbk[:, c0:512], lhsT=l_ap, rhs=r_ap, start=True, stop=(r < 0))]
                if r >= 0:
                    fns.append(lambda e, sbk=sbk, c0=c0: e.matmul(
                        sbk[:, c0:c0 + 128], lhsT=ident_bf[:], rhs=mneg_bf[:], start=False, stop=True))
                PE.group(fns, outs=[sbb], ins=[cd["kq"][(kslot, j // 4)], cd["kq"][(qslot, qb)], const_b, const2_b])
                return sbk, sbb, c0

            pend_qk = [qk(0), qk(1)]
            obk = obb = lbk = lbb = None
            for i, (qb, j) in enumerate(blocks):
                nj = 4 * qb + 4
                if j == 0:
                    obk, obb = rO.get()
                    lbk, lbb = rL.get()
                sbk, sbb, c0 = pend_qk.pop(0)
                if i + 2 < len(blocks):
                    pend_qk.append(qk(i + 2))
                p, pbuf = pring.get()
                bi = c.bidx[(qb, j)]
                act(p[:, c0:512], sbk[:, c0:512], AF.Exp, [pbuf], [sbb, bias_b],
                    scale=scale_fox, bias=bias_all[:, bi, h:h + 1])
                fns = [lambda e, obk=obk, c0=c0, j=j, p=p, nj=nj, h=h: e.matmul(
                           obk[:, c0:512], lhsT=Vv[:, j, h * 128:(h + 1) * 128], rhs=p[:, c0:512],
                           start=(j == 0), stop=(j == nj - 1)),
                       lambda e, lbk=lbk, c0=c0, j=j, p=p, nj=nj: e.matmul(
                           lbk[:, c0:512], lhsT=ones_bf[:], rhs=p[:, c0:512],
                           start=(j == 0), stop=(j == nj - 1))]
                PE.group(fns, outs=[obb, lbb], ins=[pbuf, c.Vt[j], const_b])
                if j == nj - 1:
                    t1, t1b = tring.get()
                    DVE.do(lambda e, t1=t1, lbk=lbk: e.tensor_copy(out=t1[:], in_=lbk[:]), outs=[t1b], ins=[lbb])
                    DVE.do(lambda e, t1=t1: e.reciprocal(out=t1[:], in_=t1[:]), outs=[t1b], ins=[t1b])
                    DVE.do(lambda e, t1=t1, obk=obk, qb=qb, h=h: e.tensor_tensor(
                        out=y_v[:, h, qb * 512:(qb + 1) * 512], in0=obk[:], in1=t1[:], op=ALU.mult),
                        outs=[cd["y"][(h, qb)]], ins=[obb, t1b])
                yield

        def gen_merge(c, name, gate_off, first):
            w = wbr[name]
            hv = c.H.fm()
            Mv = c.G.fm()
            if first:
                c.Mt = c.G.view([(f, T) for f in range(KC) for T in range(4)])
            nxt_w = None
            for g in range(2):
                if nxt_w is not None:
                    wbp_, wgp_ = nxt_w
                    nxt_w = None
                else:
                    wbp_ = wload(w[:, g * 512:(g + 1) * 512])
                    wgp_ = wload(w_in[:, gate_off + g * 512:gate_off + (g + 1) * 512])
                if g == 0 and wnfree() >= 3:
                    nxt_w = (wload(w[:, 512:1024]), wload(w_in[:, gate_off + 512:gate_off + 1024]))
                wbp, wgp = wbp_[0:2], wgp_[0:2]
                for fl in range(4):
                    f = 4 * g + fl
                    for T in range(4):
                        gbk, gbb = ringP.get()
                        mm(gbk[:], [(wgp[0][:, kc, fl * 128:(fl + 1) * 128], hv[:, kc, T * 512:(T + 1) * 512])
                                    for kc in range(KC)], [gbb], [wgp[1]] + hT_T(c, T))
                        sg, sgb = tring.get()
                        act(sg[:], gbk[:], AF.Sigmoid, [sgb], [gbb])
                        pbk, pbb = ringP.get()
                        mm(pbk[:], [(wbp[0][:, kc, fl * 128:(fl + 1) * 128], y_v[:, kc, T * 512:(T + 1) * 512])
                                    for kc in range(KC)], [pbb], [wbp[1]] + [cd["y"][(kc, T)] for kc in range(KC)])
                        mdst = Mv[:, f, T * 512:(T + 1) * 512]
                        mb = c.Mt[(f, T)]
                        if first:
                            DVE.do(lambda e, mdst=mdst, pbk=pbk, sg=sg: e.tensor_tensor(
                                out=mdst, in0=pbk[:], in1=sg[:], op=ALU.mult), outs=[mb], ins=[pbb, sgb])
                        else:
                            DVE.do(lambda e, pbk=pbk, sg=sg: e.tensor_tensor(
                                out=sg[:], in0=pbk[:], in1=sg[:], op=ALU.mult), outs=[sgb], ins=[pbb])
                            DVE.do(lambda e, mdst=mdst, sg=sg: e.tensor_tensor(
                                out=mdst, in0=sg[:], in1=mdst, op=ALU.add), outs=[mb], ins=[sgb])
                        yield
                wfree(wbp_)
                wfree(wgp_)

        def gen_conv(c):
            hv = c.H.fm()
            nxt_wc = None
            for f in range(KC):
                wc_ = nxt_wc if nxt_wc is not None else wload3([OFF_CC + f * 128, OFF_CV + f * 128, OFF_CB + f * 128])
                nxt_wc = None
                if f + 1 < KC and wnfree() >= 1:
                    nxt_wc = wload3([OFF_CC + (f + 1) * 128, OFF_CV + (f + 1) * 128, OFF_CB + (f + 1) * 128])
                wb, wbb = wc_[0:2]
                uprev = None
                for T in range(4):
                    cols_T = slice(T * 512, (T + 1) * 512)
                    ccbk, ccbb = ringP.get()
                    mm(ccbk[:], [(wb[:, kc, 0:128], hv[:, kc, cols_T]) for kc in range(KC)], [ccbb], [wbb] + hT_T(c, T))
                    ccs, ccsb = tring.get()
                    act(ccs[:], ccbk[:], AF.Copy, [ccsb], [ccbb])
                    cvbk, cvbb = ringP.get()
                    mm(cvbk[:], [(wb[:, kc, 128:256], hv[:, kc, cols_T]) for kc in range(KC)], [cvbb], [wbb] + hT_T(c, T))
                    u, ub = uring.get()
                    if uprev is None:
                        DVE.do(lambda e, u=u: e.memset(u[:, 0:2], 0.0), outs=[ub], ins=[])
                    else:
                        up, upb = uprev
                        DVE.do(lambda e, u=u, up=up: e.tensor_copy(out=u[:, 0:2], in_=up[:, 512:514]), outs=[ub], ins=[upb])
                    DVE.do(lambda e, u=u, cvbk=cvbk, ccs=ccs: e.tensor_tensor(
                        out=u[:, 2:514], in0=cvbk[:], in1=ccs[:], op=ALU.mult), outs=[ub], ins=[cvbb, ccsb])
                    uprev = (u, ub)
                    ta, tab = tring.get()
                    DVE.do(lambda e, ta=ta, u=u, f=f: e.tensor_scalar(
                        out=ta[:], in0=u[:, 2:514], scalar1=col(16 + f), scalar2=col(24 + f),
                        op0=ALU.mult, op1=ALU.add), outs=[tab], ins=[ub, cols_b])
                    DVE.do(lambda e, ta=ta, u=u, f=f: e.scalar_tensor_tensor(
                        out=ta[:], in0=u[:, 1:513], scalar=col(8 + f), in1=ta[:], op0=ALU.mult, op1=ALU.add),
                        outs=[tab], ins=[ub, cols_b])
                    DVE.do(lambda e, ta=ta, u=u, f=f: e.scalar_tensor_tensor(
                        out=ta[:], in0=u[:, 0:512], scalar=col(0 + f), in1=ta[:], op0=ALU.mult, op1=ALU.add),
                        outs=[tab], ins=[ub, cols_b])
                    cbbk, cbbb = ringP.get()
                    mm(cbbk[:], [(wb[:, kc, 256:384], hv[:, kc, cols_T]) for kc in range(KC)], [cbbb], [wbb] + hT_T(c, T))
                    DVE.do(lambda e, ta=ta, cbbk=cbbk, f=f, cols_T=cols_T: e.tensor_tensor(
                        out=y_v[:, f, cols_T], in0=cbbk[:], in1=ta[:], op=ALU.mult),
                        outs=[cd["y"][(f, T)]], ins=[cbbb, tab])
                    yield
                wfree(wc_)

        scale_xa = 1.0 / 16.0
        xa_w = {}

        def gen_xa_units(c, hx):
            hv = c.H.fm()
            rP, rQ = Ring(bk[0:6]), Ring([(bank_tr32, tr_b)])
            if hx % 2 == 0:
                for w_ in xa_w.values():
                    wfree(w_)
                xa_w["q"] = wload(w_in[:, OFF_XQ + (hx // 2) * 512:OFF_XQ + (hx // 2 + 1) * 512])
            wxq = xa_w["q"][0:2]
            set_ = hx % 2
            xq = CD[:, set_ * 4096:(set_ + 1) * 4096].rearrange("p (a b) -> p a b", a=2)
            def proj(T, c_):
                cols_T = slice(T * 512, (T + 1) * 512)
                fl = (2 * hx + c_) % 4
                pbk, pbb = rP.get()
                mm(pbk[:], [(wxq[0][:, kc, fl * 128:(fl + 1) * 128], hv[:, kc, cols_T]) for kc in range(KC)],
                   [pbb], [wxq[1]] + hT_T(c, T))
                sq, sqb = sqring.get()
                act(sq[:], pbk[:], AF.Square, [sqb], [pbb])
                return pbk, pbb, sq, sqb

            def finish(T, pcs):
                xqb = [cd["kq"][(2 * set_, T)], cd["kq"][(2 * set_ + 1, T)]]
                cols_T = slice(T * 512, (T + 1) * 512)
                qb_, qbb = rQ.get()
                mm(qb_[:], [(ones_bf[:], pcs[c_][2][:]) for c_ in range(2)], [qbb], [pcs[0][3], pcs[1][3], const_b])
                t1, t1b = tring.get()
                act(t1[:], qb_[:], AF.Ln, [t1b], [qbb], scale=1.0 / 256, bias=EPS)
                act(t1[:], t1[:], AF.Exp, [t1b], [t1b], scale=-0.5)
                for c_ in range(2):
                    pbk, pbb = pcs[c_][0:2]
                    DVE.do(lambda e, pbk=pbk, c_=c_, t1=t1, cols_T=cols_T: e.scalar_tensor_tensor(
                        out=xq[:, c_, cols_T], in0=pbk[:], scalar=col(34 + c_), in1=t1[:],
                        op0=ALU.mult, op1=ALU.mult), outs=[xqb[c_]], ins=[pbb, t1b, cols_b])

            pend = None
            for T in range(4):
                p0 = proj(T, 0)
                if pend is not None:
                    finish(T - 1, pend)
                p1 = proj(T, 1)
                pend = [p0, p1]
                yield
            finish(3, pend)
            yield
            if hx == 3:
                for w_ in xa_w.values():
                    wfree(w_)
                xa_w.clear()

        def gen_xa_attn(c, hx):
            set_ = hx % 2
            xq = CD[:, set_ * 4096:(set_ + 1) * 4096].rearrange("p (a b) -> p a b", a=2)
            rS, rO, rL = xa_rings

            def smm(T):
                xqb = [cd["kq"][(2 * set_, T)], cd["kq"][(2 * set_ + 1, T)]]
                cols_T = slice(T * 512, (T + 1) * 512)
                ps_ = []
                for mc in range(2):
                    sbk, sbb = rS.get()
                    mm(sbk[:], [(xa_kT[:, 2 * hx + c_, mc * 128:(mc + 1) * 128], xq[:, c_, cols_T]) for c_ in range(2)],
                       [sbb], [xak_b, xqb[0], xqb[1]])
                    p, pbuf = pring.get()
                    act(p[:], sbk[:], AF.Exp, [pbuf], [sbb], scale=scale_xa)
                    ps_.append((p, pbuf))
                return ps_

            cur = smm(0)
            for T in range(4):
                cols_T = slice(T * 512, (T + 1) * 512)
                ps_ = cur
                cur = smm(T + 1) if T + 1 < 4 else None
                lbk, lbb = rL.get()
                mm(lbk[:], [(ones_bf[:], ps_[mc][0][:]) for mc in range(2)], [lbb], [ps_[0][1], ps_[1][1], const_b])
                t1, t1b = tring.get()
                act(t1[:], lbk[:], AF.Ln, [t1b], [lbb])
                act(t1[:], t1[:], AF.Exp, [t1b], [t1b], scale=-1.0)
                evac = []
                for c_ in range(2):
                    f = 2 * hx + c_
                    obk, obb = rO.get()
                    mm(obk[:], [(xa_V[:, mc, f * 128:(f + 1) * 128], ps_[mc][0][:]) for mc in range(2)],
                       [obb], [xav_b, ps_[0][1], ps_[1][1]])
                    t2, t2b = tring.get()
                    DVE.do(lambda e, t2=t2, obk=obk: e.tensor_copy(out=t2[:], in_=obk[:]), outs=[t2b], ins=[obb])
                    evac.append((f, t2, t2b))
                for f, t2, t2b in evac:
                    DVE.do(lambda e, t1=t1, t2=t2, f=f, cols_T=cols_T: e.tensor_tensor(
                        out=y_v[:, f, cols_T], in0=t2[:], in1=t1[:], op=ALU.mult),
                        outs=[cd["y"][(f, T)]], ins=[t2b, t1b])
                yield

        wo_w = {}

        def gen_wo(c, tiles):
            b = c.b
            Mv = c.G.fm()
            hv = c.H.fm()
            load_gain(n2g_d)
            wo_c = [wload(wo_d[:, cc_ * 512:(cc_ + 1) * 512]) for cc_ in range(2)]

            def wo_load(t):
                xt, xtb, slot = xring.get()
                r0 = c.xrow0 + t * 128
                SP.dma(lambda e, xt=xt, r0=r0: e.dma_start(out=xt[:], in_=x_d[r0:r0 + 128, :]), slot, outs=[xtb])
                return xt, xtb, slot

            tiles = list(tiles)
            n = len(tiles)
            mm_res = {}
            xs = {}
            sts = {}
            xns = {}

            def do_mm(t):
                res = []
                for half in range(2):
                    pbk, pbb = ringP.get()
                    mm(pbk[:], [(Mv[:, kc, t * 128:(t + 1) * 128], wo_c[half][0][:, kc, :]) for kc in range(KC)],
                       [pbb], [wo_c[half][1]] + [c.Mt[(kc, t // 4)] for kc in range(KC)])
                    res.append((pbk, pbb))
                mm_res[t] = res

            def stage1(t):
                xt, xtb, slot = xs[t]
                r0 = c.xrow0 + t * 128
                for half in range(2):
                    pbk, pbb = mm_res[t][half]
                    DVE.do(lambda e, xt=xt, pbk=pbk, half=half: e.tensor_tensor(
                        out=xt[:, half * 512:(half + 1) * 512], in0=pbk[:], in1=xt[:, half * 512:(half + 1) * 512],
                        op=ALU.add), outs=[xtb], ins=[pbb])
                x1b = Buf()
                x1d_t[(b, t)] = x1b
                SP.dma(lambda e, xt=xt, r0=r0: e.dma_start(out=out_d[r0:r0 + 128, :], in_=xt[:]), slot,
                       outs=[x1b], ins=[xtb])
                sts[t] = norm_A1(xt, xtb)

            xs[tiles[0]] = wo_load(tiles[0])
            if n > 1:
                xs[tiles[1]] = wo_load(tiles[1])
            do_mm(tiles[0])
            if n > 1:
                do_mm(tiles[1])
            stage1(tiles[0])
            for i, t in enumerate(tiles):
                if i + 2 < n:
                    xs[tiles[i + 2]] = wo_load(tiles[i + 2])
                    do_mm(tiles[i + 2])
                if i + 1 < n:
                    stage1(tiles[i + 1])
                xt, xtb, slot = xs[t]
                st, stb = sts[t]
                xns[t] = norm_A2(xt, xtb, st, stb)
                if i >= 1:
                    tp = tiles[i - 1]
                    norm_B(xns[tp][0], xns[tp][1], hv, [c.hT[tp]], tp * 128)
                yield
            tp = tiles[-1]
            norm_B(xns[tp][0], xns[tp][1], hv, [c.hT[tp]], tp * 128)
            for w_ in wo_c:
                wfree(w_)

        def ffn_start(c):
            ainit = retire(list(cd["y"].values())) + retire(list(cd["kq"].values())) + retire(list(cd["aT"].values())) + retire([cd["memT"]])
            cd["aT"] = {(f, T2): Buf(init=ainit) for f in range(NF) for T2 in range(2)}

        def ld_out(half_, rg):
            kcn = 8 if rg < 2 else 6
            return wload(wfo[rg * 1024:rg * 1024 + kcn * 128, half_ * 512:(half_ + 1) * 512], kcn=kcn)

        def gen_ffn_in(c, hb):
            hv = c.H.fm()
            tok0 = hb * 1024
            ffn_start(c)
            aT_t = cd["aT"]
            order = (5, 0, 1, 2, 3, 4)

            def ld(ci):
                ncols_ = (4 if ci < 5 else 2) * 128
                return (wload(wfi[:, ci * 512:ci * 512 + ncols_], ncols=ncols_),
                        wload(wfi[:, DFF + ci * 512:DFF + ci * 512 + ncols_], ncols=ncols_))

            nxt_p = None
            for oi, ci in enumerate(order):
                nfl = 4 if ci < 5 else 2
                ncols = nfl * 128
                wg, wu = nxt_p if nxt_p is not None else ld(ci)
                nxt_p = None
                if oi + 1 < len(order) and wnfree() >= 2:
                    nxt_p = ld(order[oi + 1])
                elif oi + 1 == len(order):
                    c.pre_out = {}
                    for rg in range(3):
                        if wnfree() >= 1:
                            c.pre_out[(0, rg)] = ld_out(0, rg)
                for fl in range(nfl):
                    f = 4 * ci + fl
                    for T2 in range(2):
                        tc = slice(tok0 + T2 * 512, tok0 + (T2 + 1) * 512)
                        t0_ = (tok0 // 128) + 4 * T2
                        hts = [c.hT[t] for t in range(t0_, t0_ + 4)]
                        gbk, gbb = ringP.get()
                        mm(gbk[:], [(wg[0][:, kc, fl * 128:(fl + 1) * 128], hv[:, kc, tc]) for kc in range(KC)],
                           [gbb], [wg[1]] + hts)
                        sg, sgb = tring.get()
                        act(sg[:], gbk[:], AF.Silu, [sgb], [gbb])
                        ubk, ubb = ringP.get()
                        mm(ubk[:], [(wu[0][:, kc, fl * 128:(fl + 1) * 128], hv[:, kc, tc]) for kc in range(KC)],
                           [ubb], [wu[1]] + hts)
                        DVE.do(lambda e, ubk=ubk, sg=sg, f=f, T2=T2: e.tensor_tensor(
                            out=aT_v[:, f, T2 * 512:(T2 + 1) * 512], in0=ubk[:], in1=sg[:], op=ALU.mult),
                            outs=[aT_t[(f, T2)]], ins=[ubb, sgb])
                        yield
                wfree(wg)
                wfree(wu)

        def gen_ffn_out(c, hb):
            b = c.b
            aT_t = cd["aT"]
            pre_out = getattr(c, "pre_out", None) or {}
            c.pre_out = None
            for half in range(2):
                wch = [pre_out.pop((half, rg)) if (half, rg) in pre_out else ld_out(half, rg) for rg in range(3)]
                if half == 0:
                    for rg in range(2):
                        if wnfree() >= 1:
                            pre_out[(1, rg)] = ld_out(1, rg)
                    s_x = wfo[2048:2048 + 6 * 128, 512:1024].rearrange("(kc p) n -> p kc n", p=128)
                    POOL.dma(lambda e, s_x=s_x: e.dma_start(out=wx_v[:, 0:6, :], in_=s_x), wx_slot,
                             outs=[wx_b], extra=xak_b.all_toks() + xav_b.all_toks())
                    pre_out[(1, 2)] = (wx_v, wx_b, -1)

                def f_load(tl, half=half):
                    t = hb * 8 + tl
                    r0 = c.xrow0 + t * 128
                    xt, xtb, slot = xring.get()
                    SP.dma(lambda e, xt=xt, r0=r0, half=half: e.dma_start(
                        out=xt[:, 0:512], in_=out_d[r0:r0 + 128, half * 512:(half + 1) * 512]), slot,
                        outs=[xtb], ins=[x1d_t[(b, t)]])
                    return xt, xtb, slot

                nxt_x = f_load(0)
                for tl in range(8):
                    t = hb * 8 + tl
                    r0 = c.xrow0 + t * 128
                    xt, xtb, slot = nxt_x
                    if tl + 1 < 8:
                        nxt_x = f_load(tl + 1)
                    pbk, pbb = ringP.get()
                    for rg in range(3):
                        fs = range(8 * rg, min(8 * rg + 8, NF))
                        mm(pbk[:], [(aT_v[:, f, tl * 128:(tl + 1) * 128], wch[rg][0][:, f % 8, :]) for f in fs],
                           [pbb], [wch[rg][1]] + [aT_t[(f, tl // 4)] for f in fs], start=(rg == 0), stop=(rg == 2))
                    DVE.do(lambda e, xt=xt, pbk=pbk: e.tensor_tensor(
                        out=xt[:, 0:512], in0=pbk[:], in1=xt[:, 0:512], op=ALU.add), outs=[xtb], ins=[pbb])
                    ot = SP.dma(lambda e, xt=xt, r0=r0, half=half: e.dma_start(
                        out=out_d[r0:r0 + 128, half * 512:(half + 1) * 512], in_=xt[:, 0:512]), slot,
                        outs=[x1d_t[(b, t)]], ins=[xtb])
                    out_toks.append(ot)
                    yield
                for w_ in wch:
                    if w_[2] >= 0:
                        wfree(w_)

        def xa_start(c):
            kqi = retire(list(cd["kq"].values())) + retire([cd["memT"]])
            cd["kq"] = {(s_, T): Buf(init=kqi) for s_ in range(4) for T in range(4)}

        ctxs = [make_ctx(b) for b in range(NB)]
        for b in range(NB):
            c = ctxs[b]
            if b == 0:
                n1 = gen_N1(c)
                next(n1)
                next(n1)
                next(n1)
                next(n1)
                fv0 = gen_FV(c)
                next(fv0)
                late_consts()
                interleave(n1, fv0, every=1)
                late_consts_finish()
            else:
                run(gen_FV(c))
            run(gen_FV_final(c))
            fox_start(c)
            for h in range(8):
                run(gen_fox_units(c, h))
                run(gen_fox_attn(c, h))
            interleave(gen_merge(c, "fox", OFF_GB, True), gen_M(c), every=3)
            run(gen_conv(c))
            run(gen_merge(c, "conv", OFF_GA, False))
            xa_start(c)
            for hx in range(4):
                run(gen_xa_units(c, hx))
                run(gen_xa_attn(c, hx))
            run(gen_merge(c, "xa", OFF_GC, False))
            run(gen_wo(c, range(NT)))
            nxt_n1 = gen_N1(ctxs[b + 1]) if b + 1 < NB else None
            interleave(chain(gen_ffn_in(c, 0), gen_ffn_out(c, 0), gen_ffn_in(c, 1), gen_ffn_out(c, 1)), nxt_n1, every=6, lag=3)


        SP.wait(out_toks)
        for it in wring.items:
            POOL.wait(it[1].all_toks())
        POOL.wait(wf_b.all_toks())

        with nc.Block() as block:
            @block.sync
            def _(e):
                SP.replay(e)

            @block.scalar
            def _(e):
                ACT.replay(e)

            @block.vector
            def _(e):
                DVE.replay(e)

            @block.tensor
            def _(e):
                PE.replay(e)

            @block.gpsimd
            def _(e):
                POOL.replay(e)
    return nc


_NC_CACHE = {}


def _consts():
    ident = np.eye(128, dtype=np.float32)
    s = np.arange(128)
    utri = (s[:, None] <= s[None, :]).astype(np.float32)
    j = s // 8
    h = s % 8
    same_h = h[:, None] == h[None, :]
    mjj = (same_h & (j[:, None] < j[None, :])).astype(np.float32)
    mjji = (same_h & (j[:, None] <= j[None, :])).astype(np.float32)
    mneg = np.where(s[:, None] > s[None, :], NEG, 0.0).astype(np.float32)
    return {"c_ident": ident, "c_utri": utri, "c_mjj": mjj, "c_mjji": mjji, "c_mneg": mneg}


def kernel(**inputs):
    if "nc" not in _NC_CACHE:
        _NC_CACHE["nc"] = build_program()
    nc = _NC_CACHE["nc"]
    f32 = lambda a: np.ascontiguousarray(np.asarray(a, dtype=np.float32))
    x = f32(inputs["x"])
    mem = f32(inputs["mem"])
    shared = {
        "norm1_g": f32(inputs["norm1_g"]).reshape(1, D),
        "w_in": f32(inputs["w_in"]).reshape(D, IN_COLS),
        "conv_w": f32(inputs["conv_w"]).reshape(3, D),
        "conv_b": f32(inputs["conv_b"]).reshape(1, D),
        "fox_f_bias": f32(inputs["fox_f_bias"]).reshape(1, 8),
        "fox_q_g": f32(inputs["fox_q_g"]).reshape(1, 128),
        "fox_k_g": f32(inputs["fox_k_g"]).reshape(1, 128),
        "mem_norm_g": f32(inputs["mem_norm_g"]).reshape(1, D),
        "w_mem_kv": f32(inputs["w_mem_kv"]).reshape(D, 2 * D),
        "xa_q_g": f32(inputs["xa_q_g"]).reshape(1, 256),
        "xa_k_g": f32(inputs["xa_k_g"]).reshape(1, 256),
        "w_br_conv": f32(inputs["w_br_conv"]).reshape(D, D),
        "w_br_fox": f32(inputs["w_br_fox"]).reshape(D, D),
        "w_br_xa": f32(inputs["w_br_xa"]).reshape(D, D),
        "w_o": f32(inputs["w_o"]).reshape(D, D),
        "norm2_g": f32(inputs["norm2_g"]).reshape(1, D),
        "w_ffn_in": f32(inputs["w_ffn_in"]).reshape(D, 2 * DFF),
        "w_ffn_out": f32(inputs["w_ffn_out"]).reshape(DFF, D),
    }
    shared.update(_consts())
    in_maps = []
    for c in range(N_CORES):
        m = dict(shared)
        m["x"] = np.ascontiguousarray(x[NB * c:NB * (c + 1)].reshape(NB * S, D))
        m["mem"] = np.ascontiguousarray(mem[NB * c:NB * (c + 1)].reshape(NB * NMEM, D))
        in_maps.append(m)
    res = run_bass_kernel_spmd(nc, in_maps, core_ids=list(range(N_CORES)))
    out = np.concatenate([np.asarray(r["out"]).reshape(NB, S, D) for r in res.results], axis=0)
    return out.astype(np.float32)
```
